# Optimizing a Trainium2 kernel written in Bass

```python
import jax, jax.numpy as jnp
from jax import lax
import numpy as np

D_MODEL = 1024
BATCH = 2
SEQ = 8192
DEPTH = 4
DEC_BATCH = 4
DEC_SEQ = 4096
PAST_LEN = 128

N_META = 16
N_HEADS = 16
N_KV_HEADS = 4
HEAD_DIM = 64
GQA_GROUP = N_HEADS // N_KV_HEADS
ATTN_WIDTH = N_HEADS * HEAD_DIM
KV_WIDTH = N_KV_HEADS * HEAD_DIM
WINDOW = 128
BLOCK = 128
POOL_WINDOWS = (2, 4, 8, 16)
POOL_GROUPS = len(POOL_WINDOWS)
POOL_GROUP_WIDTH = 128
POOL_WIDTH = POOL_GROUPS * POOL_GROUP_WIDTH
N_BRANCHES = 2
IN_WIDTH = ATTN_WIDTH + 2 * KV_WIDTH + POOL_WIDTH + N_BRANCHES * D_MODEL
D_FF = 4 * D_MODEL
LN_EPS = 1e-5
DEEPNORM_ALPHA = float((2 * DEPTH) ** 0.25)
DEEPNORM_BETA = float((8 * DEPTH) ** -0.25)

kernel_name = "hybrid_swa_pool_deepnorm_encoder"

F32 = jnp.float32


def layer_norm(x, g, b):
    xf = x.astype(F32)
    mu = jnp.mean(xf, axis=-1, keepdims=True)
    var = jnp.mean(jnp.square(xf - mu), axis=-1, keepdims=True)
    y = (xf - mu) * lax.rsqrt(var + LN_EPS)
    return (y * g.astype(F32) + b.astype(F32)).astype(x.dtype)


def alibi_slopes():
    return jnp.asarray(2.0 ** (-8.0 * np.arange(1, N_HEADS + 1) / N_HEADS), dtype=F32)


def windowed_gqa(q, k, v, sink):
    B, L = q.shape[0], q.shape[1]
    S = L - N_META
    nb = S // BLOCK
    scale = HEAD_DIM ** -0.5
    slopes = alibi_slopes().reshape(N_KV_HEADS, GQA_GROUP)
    sink_kg = sink.astype(F32).reshape(N_KV_HEADS, GQA_GROUP)
    qm, qr = q[:, :N_META], q[:, N_META:]
    km, kr = k[:, :N_META], k[:, N_META:]
    vm, vr = v[:, :N_META], v[:, N_META:]

    qb = qr.reshape(B, nb, BLOCK, N_KV_HEADS, GQA_GROUP, HEAD_DIM)
    pad = ((0, 0), (BLOCK, BLOCK), (0, 0), (0, 0))
    kp = jnp.pad(kr, pad).reshape(B, nb + 2, BLOCK, N_KV_HEADS, HEAD_DIM)
    vp = jnp.pad(vr, pad).reshape(B, nb + 2, BLOCK, N_KV_HEADS, HEAD_DIM)
    kband = jnp.concatenate([kp[:, :-2], kp[:, 1:-1], kp[:, 2:]], axis=2)
    vband = jnp.concatenate([vp[:, :-2], vp[:, 1:-1], vp[:, 2:]], axis=2)
    s_band = jnp.einsum('bnqkgd,bnskd->bnkgqs', qb, kband, preferred_element_type=F32) * scale
    qi = jnp.arange(BLOCK)
    si = jnp.arange(3 * BLOCK)
    rel = si[None, :] - BLOCK - qi[:, None]
    key_idx = jnp.arange(nb)[:, None] * BLOCK - BLOCK + si[None, :]
    valid = (jnp.abs(rel) <= WINDOW)[None] & ((key_idx >= 0) & (key_idx < S))[:, None, :]
    bias = -slopes[:, :, None, None] * jnp.abs(rel).astype(F32)
    s_band = jnp.where(valid[None, :, None, None], s_band + bias[None, None], -jnp.inf)
    s_meta = jnp.einsum('bnqkgd,bmkd->bnkgqm', qb, km, preferred_element_type=F32) * scale
    sink_r = jnp.broadcast_to(sink_kg[None, None, :, :, None, None],
                              (B, nb, N_KV_HEADS, GQA_GROUP, BLOCK, 1))
    p = jax.nn.softmax(jnp.concatenate([s_meta, s_band, sink_r], axis=-1), axis=-1)
    pm = p[..., :N_META].astype(v.dtype)
    pb = p[..., N_META:N_META + 3 * BLOCK].astype(v.dtype)
    out_r = (jnp.einsum('bnkgqm,bmkd->bnqkgd', pm, vm)
             + jnp.einsum('bnkgqs,bnskd->bnqkgd', pb, vband)).reshape(B, S, ATTN_WIDTH)

    qmg = qm.reshape(B, N_META, N_KV_HEADS, GQA_GROUP, HEAD_DIM)
    kf, vf = kr[:, :BLOCK], vr[:, :BLOCK]
    s_mm = jnp.einsum('bqkgd,bmkd->bkgqm', qmg, km, preferred_element_type=F32) * scale
    s_mr = jnp.einsum('bqkgd,bskd->bkgqs', qmg, kf, preferred_element_type=F32) * scale
    dist = (N_META + jnp.arange(BLOCK))[None, :] - jnp.arange(N_META)[:, None]
    s_mr = jnp.where((dist <= WINDOW)[None, None, None],
                     s_mr - slopes[None, :, :, None, None] * dist.astype(F32)[None, None, None], -jnp.inf)
    sink_m = jnp.broadcast_to(sink_kg[None, :, :, None, None], (B, N_KV_HEADS, GQA_GROUP, N_META, 1))
    pmq = jax.nn.softmax(jnp.concatenate([s_mm, s_mr, sink_m], axis=-1), axis=-1)
    out_m = (jnp.einsum('bkgqm,bmkd->bqkgd', pmq[..., :N_META].astype(v.dtype), vm)
             + jnp.einsum('bkgqs,bskd->bqkgd', pmq[..., N_META:N_META + BLOCK].astype(v.dtype), vf)
             ).reshape(B, N_META, ATTN_WIDTH)
    return jnp.concatenate([out_m, out_r], axis=1)


def multiscale_pool(u, w_grp, pool_scale):
    B, L, _ = u.shape
    uf = u.astype(F32)
    cs = jnp.pad(jnp.cumsum(uf, axis=1), ((0, 0), (1, 0), (0, 0)))
    t = jnp.arange(L)
    diffs = []
    for g, w in enumerate(POOL_WINDOWS):
        lo = jnp.clip(t - w // 2, 0, L)
        hi = jnp.clip(t + w // 2, 0, L)
        sl = slice(g * POOL_GROUP_WIDTH, (g + 1) * POOL_GROUP_WIDTH)
        csg = cs[..., sl]
        mean = (jnp.take(csg, hi, axis=1) - jnp.take(csg, lo, axis=1)) / (hi - lo).astype(F32)[None, :, None]
        diffs.append(mean - uf[..., sl])
    d = jnp.stack(diffs, axis=2).astype(u.dtype)
    y = jnp.einsum('blgc,gcd->blgd', d, w_grp).reshape(B, L, POOL_WIDTH)
    return y * pool_scale


def encoder_layer(x, w_in, sink, w_pool, pool_scale, w_bo_attn, w_bo_pool, w_out, ln1_g, ln1_b,
                  w_mlp1, b_mlp1, w_mlp2, b_mlp2, ln2_g, ln2_b):
    B, L, _ = x.shape
    proj = x @ w_in
    q, k, v, u, gates = jnp.split(
        proj, [ATTN_WIDTH, ATTN_WIDTH + KV_WIDTH, ATTN_WIDTH + 2 * KV_WIDTH,
               ATTN_WIDTH + 2 * KV_WIDTH + POOL_WIDTH], axis=-1)
    q = q.reshape(B, L, N_HEADS, HEAD_DIM)
    k = k.reshape(B, L, N_KV_HEADS, HEAD_DIM)
    v = v.reshape(B, L, N_KV_HEADS, HEAD_DIM)
    ya = windowed_gqa(q, k, v, sink) @ w_bo_attn
    yb = multiscale_pool(u, w_pool, pool_scale) @ w_bo_pool
    g = jax.nn.sigmoid(gates.astype(F32)).astype(x.dtype)
    mixed = (g[..., :D_MODEL] * ya + g[..., D_MODEL:] * yb) @ w_out
    x = layer_norm(DEEPNORM_ALPHA * x + mixed, ln1_g, ln1_b)
    h = jnp.square(jax.nn.relu(x @ w_mlp1 + b_mlp1))
    return layer_norm(DEEPNORM_ALPHA * x + (h @ w_mlp2 + b_mlp2), ln2_g, ln2_b)


def encode(x, meta_tokens, ln_emb_g, ln_emb_b, layer_params):
    B = x.shape[0]
    meta = jnp.broadcast_to(meta_tokens.astype(x.dtype)[None], (B, N_META, D_MODEL))
    h = layer_norm(jnp.concatenate([meta, x], axis=1), ln_emb_g, ln_emb_b)
    for l in range(DEPTH):
        h = encoder_layer(h, *[p[l] for p in layer_params])
    return h[:, N_META:]


def setup_inputs(seed: int = 0) -> dict:
    key = jax.random.key(seed)
    ks = jax.random.split(key, 24)
    nrm = lambda k, shape, s: jax.random.normal(k, shape, dtype=F32) * s
    return {
        "x_prompt": nrm(ks[0], (BATCH, SEQ, D_MODEL), 1.0),
        "x_sample": nrm(ks[1], (DEC_BATCH, DEC_SEQ, D_MODEL), 1.0),
        "meta_tokens": nrm(ks[2], (N_META, D_MODEL), 1.0),
        "ln_emb_g": 1.0 + nrm(ks[3], (D_MODEL,), 0.02),
        "ln_emb_b": nrm(ks[4], (D_MODEL,), 0.02),
        "w_in": nrm(ks[5], (DEPTH, D_MODEL, IN_WIDTH), D_MODEL ** -0.5),
        "sink": nrm(ks[6], (DEPTH, N_HEADS), 0.5),
        "w_pool": nrm(ks[7], (DEPTH, POOL_GROUPS, POOL_GROUP_WIDTH, POOL_GROUP_WIDTH), POOL_GROUP_WIDTH ** -0.5),
        "pool_scale": 1.0 + nrm(ks[8], (DEPTH, POOL_WIDTH), 0.02),
        "w_bo_attn": nrm(ks[9], (DEPTH, ATTN_WIDTH, D_MODEL), ATTN_WIDTH ** -0.5),
        "w_bo_pool": nrm(ks[10], (DEPTH, POOL_WIDTH, D_MODEL), POOL_WIDTH ** -0.5),
        "w_out": nrm(ks[11], (DEPTH, D_MODEL, D_MODEL), DEEPNORM_BETA * D_MODEL ** -0.5),
        "ln1_g": 1.0 + nrm(ks[12], (DEPTH, D_MODEL), 0.02),
        "ln1_b": nrm(ks[13], (DEPTH, D_MODEL), 0.02),
        "w_mlp1": nrm(ks[14], (DEPTH, D_MODEL, D_FF), D_MODEL ** -0.5),
        "b_mlp1": nrm(ks[15], (DEPTH, D_FF), 0.02),
        "w_mlp2": nrm(ks[16], (DEPTH, D_FF, D_MODEL), DEEPNORM_BETA * D_FF ** -0.5),
        "b_mlp2": nrm(ks[17], (DEPTH, D_MODEL), 0.02),
        "ln2_g": 1.0 + nrm(ks[18], (DEPTH, D_MODEL), 0.02),
        "ln2_b": nrm(ks[19], (DEPTH, D_MODEL), 0.02),
    }


def reference(x_prompt, x_sample, meta_tokens, ln_emb_g, ln_emb_b, w_in, sink, w_pool, pool_scale,
              w_bo_attn, w_bo_pool, w_out, ln1_g, ln1_b, w_mlp1, b_mlp1, w_mlp2, b_mlp2, ln2_g, ln2_b):
    layer_params = (w_in, sink, w_pool, pool_scale, w_bo_attn, w_bo_pool, w_out, ln1_g, ln1_b,
                    w_mlp1, b_mlp1, w_mlp2, b_mlp2, ln2_g, ln2_b)
    y_prompt = encode(x_prompt, meta_tokens, ln_emb_g, ln_emb_b, layer_params)
    y_sample = encode(x_sample, meta_tokens, ln_emb_g, ln_emb_b, layer_params)
    return (y_prompt, y_sample)
```

```python
import contextlib
import numpy as np
import concourse.bass as bass
import concourse.mybir as mybir
from concourse.bass_utils import run_bass_kernel_spmd

F32 = mybir.dt.float32
BF16 = mybir.dt.bfloat16
ALU = mybir.AluOpType
AF = mybir.ActivationFunctionType

D = 1024
DEPTH = 4
NH = 16
NKV = 4
N_META = 16
ALPHA = float((2 * DEPTH) ** 0.25)
LN_EPS = 1e-5
NEG = -30000.0
POOL_W = (2, 4, 8, 16)
NCH_A = 15
NCH_M = 16
NCH = NCH_A + NCH_M
CH = 4096
ENGS = ("pe", "act", "dve", "pool", "sp")
EPOCH = 16000
import os
USE_FP32_STREAM = bool(os.environ.get("FP32STREAM"))


class Sched:
    def __init__(self):
        self.ops = {e: [] for e in ENGS}
        self.cnt = {e: 0 for e in ENGS}
        self.known = {e: {} for e in ENGS}
        self.lastw = {}
        self.readers = {}
        self.dmacnt = {}
        self.snap = {}
        self.nwaits = 0

    def _deps(self, eng, reads, writes, is_dma):
        deps = {}

        def add(p, kind):
            if (not is_dma) and p[0] == "e" and p[1] == eng and kind != "raw":
                return
            k = (p[0], p[1])
            if deps.get(k, 0) < p[2]:
                deps[k] = p[2]

        for r in reads:
            w = self.lastw.get(r)
            if w is not None:
                add(w, "raw")
        for w_ in writes:
            w = self.lastw.get(w_)
            if w is not None:
                add(w, "waw")
            for rd in self.readers.get(w_, ()):
                add(rd, "war")
        return deps

    def _filter(self, eng, deps):
        kn = self.known[eng]
        waits = []
        for k, v in deps.items():
            if kn.get(k, 0) >= v:
                continue
            waits.append((k, v))
        for k, v in waits:
            if kn.get(k, 0) < v:
                kn[k] = v
            sn = self.snap.get((k[0], k[1], v))
            if sn:
                for k2, v2 in sn.items():
                    if kn.get(k2, 0) < v2:
                        kn[k2] = v2
        self.nwaits += len(waits)
        return waits

    def _update(self, me, reads, writes):
        for r in reads:
            self.readers.setdefault(r, []).append(me)
        for w_ in writes:
            self.lastw[w_] = me
            self.readers[w_] = []

    def op(self, eng, fn, reads=(), writes=()):
        deps = self._deps(eng, reads, writes, False)
        waits = self._filter(eng, deps)
        self.cnt[eng] += 1
        seq = self.cnt[eng]
        self.ops[eng].append(("op", waits, fn, seq))
        self.snap[("e", eng, seq)] = dict(self.known[eng])
        self._update(("e", eng, seq), reads, writes)

    def dma(self, q, fn, reads, writes, key, serialize=False):
        deps = self._deps(q, reads, writes, True)
        if serialize and self.dmacnt.get(key, 0) > 0:
            deps[("d", key)] = max(deps.get(("d", key), 0), self.dmacnt[key])
        waits = self._filter(q, deps)
        v = self.dmacnt.get(key, 0) + 16
        self.dmacnt[key] = v
        self.ops[q].append(("dma", waits, fn, key))
        self.snap[("d", key, v)] = dict(self.known[q])
        self._update(("d", key, v), reads, writes)

    def final_wait(self, eng):
        waits = []
        for key, v in self.dmacnt.items():
            if self.known[eng].get(("d", key), 0) < v:
                waits.append((("d", key), v))
        self.ops[eng].append(("wait", waits, None, None))

    def emit(self, nc, block, stack):
        esem = {}
        for e in ENGS:
            n = (self.cnt[e] + EPOCH - 1) // EPOCH
            esem[e] = [stack.enter_context(nc.semaphore(f"s_{e}_{i}")) for i in range(max(n, 1))]
        dsem = {}
        for i, key in enumerate(self.dmacnt):
            dsem[key] = stack.enter_context(nc.semaphore(f"d_{i}"))

        def do_waits(eng, waits):
            for (k, v) in waits:
                if k[0] == "e":
                    eng.wait_ge(esem[k[1]][(v - 1) // EPOCH], (v - 1) % EPOCH + 1)
                else:
                    eng.wait_ge(dsem[k[1]], v)

        def run(ename):
            def body(eng):
                for rec in self.ops[ename]:
                    kind, waits, fn, extra = rec
                    do_waits(eng, waits)
                    if kind == "op":
                        inst = fn(eng)
                        inst.then_inc(esem[ename][(extra - 1) // EPOCH], 1)
                    elif kind == "dma":
                        inst = fn(eng)
                        inst.then_inc(dsem[extra], 16)
            return body

        block.tensor(run("pe"))
        block.scalar(run("act"))
        block.vector(run("dve"))
        block.gpsimd(run("pool"))
        block.sync(run("sp"))


def build_program(S, L, debug_layers=None):
    NT = 4
    assert S % NT == 0
    ntiles = S // NT
    nc = bass.Bass("TRN2", target_bir_lowering=False)
    sc = Sched()

    def din(name, shape, dt=F32):
        return nc.dram_tensor(name, shape, dt, kind="ExternalInput").ap()

    xin = din("xin", [S * 128, D])
    wst = din("wst", [L, NCH, 128, CH])
    c_alibi = din("c_alibi", [128, 3 * 4 * 512])
    c_pen = din("c_pen", [128, S * 3])
    c_nvp = din("c_nvp", [128, S])
    c_pool = din("c_pool", [128, 1152])
    c_vec = din("c_vec", [5 * L + 2, D])
    c_b1 = din("c_b1", [128, L * 32])
    c_ps = din("c_ps", [128, L * 4])
    c_sink = din("c_sink", [1, L * 16])
    yout = nc.dram_tensor("y", [S * 128, D], F32, kind="ExternalOutput").ap()
    xs = nc.dram_tensor("xs", [S * 128, D], F32, kind="Internal").ap()
    x1s = nc.dram_tensor("x1s", [S * 128, D], F32, kind="Internal").ap()
    wbf = nc.dram_tensor("wbf", [L, NCH, 128, CH], BF16, kind="Internal").ap()

    stack = contextlib.ExitStack()
    with stack:
        def sb(name, shape, dt):
            return stack.enter_context(nc.sbuf_tensor(name, shape, dt))

        NRING = 5
        ring = [sb(f"ring{i}", [128, CH], BF16) for i in range(NRING)]
        xb = [sb(f"xb{i}", [128, D], BF16) for i in range(6)]
        XA = sb("XA", [128, 4, D], F32)
        xT = sb("xT", [128, 8, 768], BF16)
        arB = sb("arB", [128, 18432], BF16)
        KT = arB[:, 0:3072].rearrange("p (j t) -> p j t", j=4)
        QTz = arB[:, 3072:11264].rearrange("p (j i c) -> p j i c", j=4, i=4)
        U = arB[:, 11264:14336].rearrange("p (e c) -> p e c", e=6)
        MT = arB[:, 14336:18432].rearrange("p (m t) -> p m t", m=8)
        HT = arB[:, 0:16384].rearrange("p (m t) -> p m t", m=32)
        VA = sb("VA", [128, 6, 4, 128], BF16)
        PT = [sb(f"PT{i}", [128, 4, 512], BF16) for i in range(3)]
        AT = sb("AT", [128, 8, 512], BF16)
        DT = sb("DT", [128, 4, 512], BF16)
        YP = sb("YP", [128, 4, 512], BF16)
        WP = sb("WP", [128, 512], BF16)
        KTm = sb("KTm", [128, 4, 16], BF16)
        VAm = sb("VAm", [128, 4, 128], BF16)
        Abl = [sb(f"Abl{i}", [128, 4, 128], BF16) for i in range(2)]
        Anx = [sb(f"Anx{i}", [128, 4, 8], BF16) for i in range(2)]
        Ast = sb("Ast", [128, 4, 128], BF16)
        Apv = sb("Apv", [128, 4, 8], BF16)
        ident = sb("ident", [128, 128], BF16)
        NTF = 3
        tmpF = [sb(f"tmpF{i}", [128, 512], F32) for i in range(NTF)]
        den = [sb(f"den{i}", [128, 4], F32) for i in range(2)]
        rcp = [sb(f"rcp{i}", [128, 4], F32) for i in range(2)]
        ATtok = [sb(f"ATtok{i}", [128, D], BF16) for i in range(2)]
        esink = sb("esink", [128, L * 16], F32)
        TH = [sb(f"TH{i}", [128, 2, 512], F32) for i in range(2)]
        TU = [sb(f"TU{i}", [128, 512], F32) for i in range(2)]
        B2t = TH[0][:, :, :].rearrange("p a b -> p (a b)")
        alibi = sb("alibi", [128, 3, 4, 512], BF16)
        pen = sb("pen", [128, S, 3], F32)
        nvp = sb("nvp", [128, S], F32)
        cpool = sb("cpool", [128, 128], F32)
        Gt = sb("Gt", [128, D], F32)
        Bt = sb("Bt", [128, D], F32)
        b1t = sb("b1t", [128, L * 32], F32)
        pst = sb("pst", [128, L * 4], F32)
        stats = sb("stats", [128, 8, 2, 6], F32)
        mv = sb("mv", [128, 8, 2], F32)
        sd = sb("sd", [128, 8], F32)
        rstd = sb("rstd", [128, 8], F32)
        nmr = sb("nmr", [128, 8], F32)
        psall = stack.enter_context(nc.psum_tensor("psall", [128, 4096], F32))
        ps = [psall[:, k * 512:(k + 1) * 512] for k in range(8)]

        state = {"xst": 0, "xb": 0, "g": 0, "tf": 0}

        def R(*a):
            return tuple(a)

        def cp4(a, b):
            return cpool[:, a:b].rearrange("p (g t) -> p g t", g=4)

        def load_consts():
            sc.dma("pool", lambda e: e.dma_start(out=alibi[:].rearrange("p a b c -> p (a b c)"), in_=c_alibi[:, :]),
                   [], [R("alibi")], "c0")
            sc.dma("sp", lambda e: e.dma_start(out=pen[:].rearrange("p a b -> p (a b)"), in_=c_pen[:, :]),
                   [], [R("pen")], "c1")
            sc.dma("sp", lambda e: e.dma_start(out=nvp[:], in_=c_nvp[:, :]), [], [R("nvp")], "c2")
            sc.dma("sp", lambda e: e.dma_start(out=cpool[:], in_=c_pool[:, 1024:1152]), [], [R("cpool")], "c3")
            sc.dma("sp", lambda e: e.dma_start(out=b1t[:], in_=c_b1[:, :]), [], [R("b1t")], "c4")
            sc.dma("sp", lambda e: e.dma_start(out=pst[:], in_=c_ps[:, :]), [], [R("pst")], "c5")
            sc.dma("sp", lambda e: e.dma_start(out=esink[:], in_=c_sink[0:1, :].partition_broadcast(128)), [], [R("esink")], "c6")
            sc.op("act", lambda e: e.activation(out=esink[:], in_=esink[:], func=AF.Exp), [R("esink")], [R("esink")])
            sc.op("pool", lambda e: e.memset(ident[:], 0.0), [], [R("ident")])
            sc.op("pool", lambda e: e.affine_select(out=ident[:], in_=ident[:], pattern=[[-1, 128]],
                                                    compare_op=ALU.not_equal, fill=1.0, base=0,
                                                    channel_multiplier=1), [R("ident")], [R("ident")])
            for b in range(3):
                sc.op("pool", lambda e, b=b: e.memset(PT[b][:, 0, :], 0.0), [], [R("PT", b)])
            sc.op("pool", lambda e: e.memset(VAm[:], 0.0), [], [R("VAm")])
            sc.op("pool", lambda e: e.memset(VAm[0:16, :, 64:128], 1.0), [R("VAm")], [R("VAm")])
            sc.op("pool", lambda e: e.memset(VA[:, :, :, 64:128], 1.0), [], [R("VA")])
            for b in range(2):
                sc.dma("pool", lambda e, b=b: e.dma_start(out=Abl[b][:].rearrange("p g t -> p (g t)"), in_=c_pool[:, 0:512]),
                       [], [R("Abl", b)], f"cA{b}")
            sc.dma("pool", lambda e: e.dma_start(out=Ast[:].rearrange("p g t -> p (g t)"), in_=c_pool[:, 512:1024]),
                   [], [R("Ast")], "cA2")
            sc.op("dve", lambda e: e.tensor_copy(out=Apv[:], in_=cp4(64, 96)), [R("cpool")], [R("Apv")])

        def load_vec(tile, row, tag):
            sc.dma("pool", lambda e: e.dma_start(out=tile[:], in_=c_vec[row:row + 1, :].partition_broadcast(128)),
                   [], [R(tag)], "v_" + tag)

        tasks = []
        for l in range(L):
            tasks += [("A", l, t) for t in range(ntiles)]
            tasks += [("M", l, t) for t in range(ntiles)]
        wsched = []
        for (k, l, t) in tasks:
            wsched += [(l, c) for c in (range(0, NCH_A) if k == "A" else range(NCH_A, NCH))]
        wst_state = {"issued": 0, "taken": 0}

        def _wissue():
            k = wst_state["issued"]
            if k >= len(wsched):
                return
            l, c = wsched[k]
            slot = k % NRING
            t = ring[slot]
            if l == 0 and c < NCH_A:
                sc.dma("pool", lambda e: e.dma_start(out=t[:], in_=wst[l, c, :, :]), [], [R("ring", slot)], f"w{slot}")
            else:
                sc.dma("sp", lambda e: e.dma_start(out=t[:], in_=wbf[l, c, :, :]), [R("wbf", l, c)], [R("ring", slot)], f"w{slot}")
            wst_state["issued"] += 1

        cv_state = {"n": 0, "next": {0: NCH_A}}

        def convert(l, n):
            if l >= L:
                return
            c0 = cv_state["next"].get(l, 0)
            for c in range(c0, min(c0 + n, NCH)):
                key = f"cv{cv_state['n'] % 8}"
                cv_state["n"] += 1
                sc.dma("pool", lambda e, c=c: e.dma_start(
                    out=wbf[l, c, :, :].rearrange("(a b) f -> a (b f)", a=16),
                    in_=wst[l, c, :, :].rearrange("(a b) f -> a (b f)", a=16)), [], [R("wbf", l, c)], key, serialize=True)
            cv_state["next"][l] = min(c0 + n, NCH)

        def wnext(l, c):
            k = wst_state["taken"]
            assert wsched[k] == (l, c), (wsched[k], l, c)
            while wst_state["issued"] <= k:
                _wissue()
            wst_state["taken"] += 1
            return ring[k % NRING], R("ring", k % NRING)

        def wprefetch():
            while wst_state["issued"] < wst_state["taken"] + NRING:
                if wst_state["issued"] >= len(wsched):
                    break
                _wissue()

        def psbank(k):
            return ps[k], R("ps", k)

        def head_load(src, positions, res_name, k0=0):
            for k, pos in zip(range(k0, k0 + len(positions)), positions):
                sc.dma("pool", lambda e, k=k, pos=pos: e.dma_start(out=xb[k][:], in_=src[pos * 128:(pos + 1) * 128, :]),
                       [R(res_name, pos)], [R("xb", k)], f"xb{k}")

        def head_tr(n, k0=0):
            for k in range(k0, n):
                pt, pr = psbank(2 + state["g"] % 4)
                state["g"] += 1
                ptb = pt.bitcast(BF16)
                for kc in range(8):
                    sc.op("pe", lambda e, kc=kc, k=k, ptb=ptb: e.transpose(out=ptb[:, kc * 128:(kc + 1) * 128],
                                                                          in_=xb[k][:, kc * 128:(kc + 1) * 128], identity=ident[:]),
                          [R("xb", k), R("ident")], [pr])
                sc.op("dve", lambda e, k=k, ptb=ptb: e.tensor_copy(out=xT[:, :, k * 128:(k + 1) * 128],
                                                                   in_=ptb.rearrange("p (k t) -> p k t", k=8)),
                      [pr], [R("xT")])

        def ln_slot(i, after_slot):
            for h in range(2):
                sc.op("dve", lambda e, h=h: e.bn_stats(out=stats[:, i, h, :], in_=XA[:, i, h * 512:(h + 1) * 512]),
                      [R("XA", i)], [R("stats", i)])
            sc.op("dve", lambda e: e.bn_aggr(out=mv[:, i, :], in_=stats[:, i, :, :].rearrange("p a b -> p (a b)")),
                  [R("stats", i)], [R("mv", i)])
            sc.op("act", lambda e: e.activation(out=sd[:, i:i + 1], in_=mv[:, i, 1:2], func=AF.Sqrt, bias=LN_EPS, scale=1.0),
                  [R("mv", i)], [R("sd", i)])
            sc.op("dve", lambda e: e.reciprocal(out=rstd[:, i:i + 1], in_=sd[:, i:i + 1]), [R("sd", i)], [R("rstd", i)])
            sc.op("dve", lambda e: e.scalar_tensor_tensor(out=nmr[:, i:i + 1], in0=mv[:, i, 0:1], scalar=-1.0,
                                                          in1=rstd[:, i:i + 1], op0=ALU.mult, op1=ALU.mult),
                  [R("mv", i), R("rstd", i)], [R("nmr", i)])
            sc.op("act", lambda e: e.activation(out=XA[:, i, :], in_=XA[:, i, :], func=AF.Identity,
                                                scale=rstd[:, i:i + 1], bias=nmr[:, i:i + 1]),
                  [R("XA", i), R("rstd", i), R("nmr", i)], [R("XA", i)])
            sc.op("dve", lambda e: e.tensor_tensor(out=XA[:, i, :], in0=XA[:, i, :], in1=Gt[:], op=ALU.mult),
                  [R("XA", i), R("Gt")], [R("XA", i)])
            sc.op("pool", lambda e: e.tensor_tensor(out=XA[:, i, :], in0=XA[:, i, :], in1=Bt[:], op=ALU.add),
                  [R("XA", i), R("Bt")], [R("XA", i)])
            after_slot(i)

        def ln_group(n, after_slot):
            for i in range(n):
                for h in range(2):
                    sc.op("dve", lambda e, i=i, h=h: e.bn_stats(out=stats[:, i, h, :], in_=XA[:, i, h * 512:(h + 1) * 512]),
                          [R("XA", i)], [R("stats", i)])
                sc.op("dve", lambda e, i=i: e.bn_aggr(out=mv[:, i, :], in_=stats[:, i, :, :].rearrange("p a b -> p (a b)")),
                      [R("stats", i)], [R("mv", i)])
            for i in range(n):
                sc.op("act", lambda e, i=i: e.activation(out=sd[:, i:i + 1], in_=mv[:, i, 1:2], func=AF.Sqrt, bias=LN_EPS, scale=1.0),
                      [R("mv", i)], [R("sd", i)])
            for i in range(n):
                sc.op("dve", lambda e, i=i: e.reciprocal(out=rstd[:, i:i + 1], in_=sd[:, i:i + 1]), [R("sd", i)], [R("rstd", i)])
                sc.op("dve", lambda e, i=i: e.scalar_tensor_tensor(out=nmr[:, i:i + 1], in0=mv[:, i, 0:1], scalar=-1.0,
                                                                   in1=rstd[:, i:i + 1], op0=ALU.mult, op1=ALU.mult),
                      [R("mv", i), R("rstd", i)], [R("nmr", i)])
            for i in range(n):
                sc.op("act", lambda e, i=i: e.activation(out=XA[:, i, :], in_=XA[:, i, :], func=AF.Identity,
                                                         scale=rstd[:, i:i + 1], bias=nmr[:, i:i + 1]),
                      [R("XA", i), R("rstd", i), R("nmr", i)], [R("XA", i)])
            for i in range(n):
                sc.op("dve", lambda e, i=i: e.tensor_tensor(out=XA[:, i, :], in0=XA[:, i, :], in1=Gt[:], op=ALU.mult),
                      [R("XA", i), R("Gt")], [R("XA", i)])
            for i in range(n):
                sc.op("pool", lambda e, i=i: e.tensor_tensor(out=XA[:, i, :], in0=XA[:, i, :], in1=Bt[:], op=ALU.add),
                      [R("XA", i), R("Bt")], [R("XA", i)])
                after_slot(i)

        def store_rows(dst, pos, i, dst_res):
            sc.dma("pool", lambda e: e.dma_start(out=dst[pos * 128:(pos + 1) * 128, :], in_=XA[:, i, :]),
                   [R("XA", i)], [dst_res], f"st{i}")

        def pass_emb():
            load_vec(Gt, 5 * L, "Gt")
            load_vec(Bt, 5 * L + 1, "Bt")
            arF = arB[:, :].bitcast(F32)
            bufs = [arF[:, k * D:(k + 1) * D] for k in range(8)]
            for t in range(ntiles):
                ks = [(t % 2) * 4 + i for i in range(NT)]
                poss = [t * NT + i for i in range(NT)]
                for k, pos in zip(ks, poss):
                    sc.dma("sp", lambda e, k=k, pos=pos: e.dma_start(out=bufs[k], in_=xin[pos * 128:(pos + 1) * 128, :]),
                           [], [R("p0", k)], f"p0l{k}")
                for k in ks:
                    for h in range(2):
                        sc.op("dve", lambda e, k=k, h=h: e.bn_stats(out=stats[:, k, h, :], in_=bufs[k][:, h * 512:(h + 1) * 512]),
                              [R("p0", k)], [R("stats", k)])
                    sc.op("dve", lambda e, k=k: e.bn_aggr(out=mv[:, k, :], in_=stats[:, k, :, :].rearrange("p a b -> p (a b)")),
                          [R("stats", k)], [R("mv", k)])
                for k in ks:
                    sc.op("act", lambda e, k=k: e.activation(out=sd[:, k:k + 1], in_=mv[:, k, 1:2], func=AF.Sqrt, bias=LN_EPS, scale=1.0),
                          [R("mv", k)], [R("sd", k)])
                for k in ks:
                    sc.op("dve", lambda e, k=k: e.reciprocal(out=rstd[:, k:k + 1], in_=sd[:, k:k + 1]), [R("sd", k)], [R("rstd", k)])
                    sc.op("dve", lambda e, k=k: e.scalar_tensor_tensor(out=nmr[:, k:k + 1], in0=mv[:, k, 0:1], scalar=-1.0,
                                                                       in1=rstd[:, k:k + 1], op0=ALU.mult, op1=ALU.mult),
                          [R("mv", k), R("rstd", k)], [R("nmr", k)])
                for k in ks:
                    sc.op("act", lambda e, k=k: e.activation(out=bufs[k], in_=bufs[k], func=AF.Identity,
                                                             scale=rstd[:, k:k + 1], bias=nmr[:, k:k + 1]),
                          [R("p0", k), R("rstd", k), R("nmr", k)], [R("p0", k)])
                for k in ks:
                    sc.op("dve", lambda e, k=k: e.tensor_tensor(out=bufs[k], in0=bufs[k], in1=Gt[:], op=ALU.mult),
                          [R("p0", k), R("Gt")], [R("p0", k)])
                for k, pos in zip(ks, poss):
                    sc.op("pool", lambda e, k=k: e.tensor_tensor(out=bufs[k], in0=bufs[k], in1=Bt[:], op=ALU.add),
                          [R("p0", k), R("Bt")], [R("p0", k)])
                    sc.dma("pool", lambda e, k=k, pos=pos: e.dma_start(out=xs[pos * 128:(pos + 1) * 128, :], in_=bufs[k]),
                           [R("p0", k)], [R("xs", pos)], f"p0s{k}")
            allp0 = [R("p0", k) for k in range(8)]
            sc.op("dve", lambda e: e.memset(stats[:, 0, 0, 0:1], 0.0), [], allp0 + [R("stats", 0)])
            sc.op("act", lambda e: e.activation(out=sd[:, 0:1], in_=sd[:, 0:1], func=AF.Copy), [], allp0 + [R("sd", 0)])

        def m_head_load(l, t):
            head_load(x1s, [t * NT + i for i in range(NT)], "x1s")

        def m_head_tr(l, t):
            head_tr(NT)

        def m_body1(l, t):
            p0 = t * NT
            if t == 0:
                load_vec(Gt, 5 * l + 2, "Gt")
                load_vec(Bt, 5 * l + 3, "Bt")
                sc.dma("pool", lambda e: e.dma_start(out=B2t, in_=c_vec[5 * l + 4:5 * l + 5, :].partition_broadcast(128)),
                       [], [R("B2t"), R("TH", 0, 0), R("TH", 0, 1)], "v_B2t")
            for c in range(8):
                W, wr = wnext(l, NCH_A + c)
                for mm in range(4):
                    m = 4 * c + mm
                    bk = [0, 1, 6, 7, 2, 3, 4, 5][state["g"] % 8]
                    state["g"] += 1
                    pt, pr = psbank(bk)
                    for kc in range(8):
                        sc.op("pe", lambda e, kc=kc, mm=mm, W=W, pt=pt: e.matmul(
                            pt[:, :], lhsT=W[:, kc * 512 + mm * 128: kc * 512 + (mm + 1) * 128],
                            rhs=xT[:, kc, 0:512], start=(kc == 0), stop=(kc == 7)),
                            [wr, R("xT")], [pr])
                    tb = state["tf"] % NTF
                    state["tf"] += 1
                    sc.op("act", lambda e, pt=pt, tb=tb, m=m: e.activation(
                        out=tmpF[tb][:], in_=pt[:, :], func=AF.Relu, bias=b1t[:, l * 32 + m: l * 32 + m + 1], scale=1.0),
                        [pr, R("b1t")], [R("tmpF", tb)])
                    sc.op("dve", lambda e, tb=tb, m=m: e.tensor_tensor(out=HT[:, m, :], in0=tmpF[tb][:], in1=tmpF[tb][:], op=ALU.mult),
                          [R("tmpF", tb)], [R("HT", m)])
                wprefetch()

        def m_body2(l, t, dst):
            p0 = t * NT
            for i in range(NT):
                pos = p0 + i
                sc.dma("pool", lambda e, pos=pos, i=i: e.dma_start(out=XA[:, i, :], in_=x1s[pos * 128:(pos + 1) * 128, :]),
                       [R("x1s", pos)], [R("XA", i)], f"xa{i}")
                sc.op("act", lambda e, i=i: e.activation(out=XA[:, i, :], in_=XA[:, i, :], func=AF.Copy, scale=ALPHA),
                      [R("XA", i)], [R("XA", i)])
                sc.op("pool", lambda e, i=i: e.tensor_tensor(out=XA[:, i, :], in0=XA[:, i, :], in1=B2t, op=ALU.add),
                      [R("XA", i), R("B2t"), R("TH", 0, 0), R("TH", 0, 1)], [R("XA", i)])
            for hf in range(2):
                for kg in range(4):
                    W, wr = wnext(l, NCH_A + 8 + hf * 4 + kg)
                    for i in range(NT):
                        pt, pr = psbank(2 + i)
                        for kc in range(8):
                            kk = kg * 8 + kc
                            sc.op("pe", lambda e, kc=kc, kk=kk, i=i, W=W, pt=pt, kg=kg: e.matmul(
                                pt[:, :], lhsT=HT[:, kk, i * 128:(i + 1) * 128], rhs=W[:, kc * 512:(kc + 1) * 512],
                                start=(kg == 0 and kc == 0), stop=(kg == 3 and kc == 7)),
                                [wr, R("HT", kk)], [pr])
                        if kg == 3:
                            sc.op("dve", lambda e, i=i, hf=hf, pt=pt: e.tensor_tensor(
                                out=XA[:, i, hf * 512:(hf + 1) * 512], in0=pt[:, :], in1=XA[:, i, hf * 512:(hf + 1) * 512], op=ALU.add),
                                [pr, R("XA", i)], [R("XA", i)])
                    wprefetch()
            ln_group(NT, lambda i: store_rows(dst, p0 + i, i, R("xs", p0 + i) if dst is xs else R("y", p0 + i)))

        def a_head_load(l, t):
            p0 = t * NT
            e0 = 0 if t == 0 else 2
            head_load(xs, [min(max(p0 - 1 + e, 0), S - 1) for e in range(e0, 6)], "xs", k0=e0)

        def a_head_tr(l, t):
            if t > 0:
                sc.op("act", lambda e: e.activation(out=xT[:, :, 0:256], in_=xT[:, :, 512:768], func=AF.Copy), [R("xT")], [R("xT")])
                sc.op("act", lambda e: e.activation(out=KT[:, :, 0:256], in_=KT[:, :, 512:768], func=AF.Copy), [R("KT")], [R("KT")])
                sc.op("pool", lambda e: e.tensor_copy(out=VA[:, 0:2, :, 0:64], in_=VA[:, 4:6, :, 0:64]),
                      [R("VA", 4), R("VA", 5)], [R("VA", 0), R("VA", 1)])
                sc.op("pool", lambda e: e.tensor_copy(out=U[:, 0:2, :], in_=U[:, 4:6, :]),
                      [R("U", 4), R("U", 5)], [R("U", 0), R("U", 1)])
            head_tr(6, k0=0 if t == 0 else 2)

        def a_body1(l, t):
            p0 = t * NT
            if t == 0:
                load_vec(Gt, 5 * l + 0, "Gt")
                load_vec(Bt, 5 * l + 1, "Bt")
                sc.op("dve", lambda e: e.memset(QTz[:, :, :, :], 0.0), [], [R("QTz", jq) for jq in range(4)])
            e0 = 0 if t == 0 else 2
            W, wr = wnext(l, 0)
            for j in range(4):
                for (t0, n) in (((0, 512), (512, 256)) if t == 0 else ((256, 512),)):
                    bk = [0, 1, 6, 7, 2, 3, 4, 5][state["g"] % 8]
                    state["g"] += 1
                    pt, pr = psbank(bk)
                    for kc in range(8):
                        sc.op("pe", lambda e, kc=kc, j=j, t0=t0, n=n, W=W, pt=pt: e.matmul(
                            pt[:, 0:n], lhsT=W[:, kc * 512 + j * 128: kc * 512 + (j + 1) * 128],
                            rhs=xT[:, kc, t0:t0 + n], start=(kc == 0), stop=(kc == 7)),
                            [wr, R("xT")], [pr])
                    sc.op("act", lambda e, j=j, t0=t0, n=n, pt=pt: e.activation(out=KT[:, j, t0:t0 + n], in_=pt[:, 0:n], func=AF.Copy),
                          [pr], [R("KT")])
            wprefetch()
            W, wr = wnext(l, 1)
            for e_ in range(e0, 6):
                bk = [0, 1, 6, 7, 2, 3, 4, 5][state["g"] % 8]
                state["g"] += 1
                pt, pr = psbank(bk)
                for kc in range(8):
                    sc.op("pe", lambda e, kc=kc, e_=e_, W=W, pt=pt: e.matmul(
                        pt[:, :], lhsT=xT[:, kc, e_ * 128:(e_ + 1) * 128], rhs=W[:, kc * 512:(kc + 1) * 512],
                        start=(kc == 0), stop=(kc == 7)), [wr, R("xT")], [pr])
                sc.op("dve", lambda e, e_=e_, pt=pt: e.tensor_copy(out=U[:, e_, :], in_=pt[:, :]), [pr], [R("U", e_)])
            wprefetch()
            W, wr = wnext(l, 2)
            sc.op("act", lambda e, W=W: e.activation(out=WP[:], in_=W[:, 2048:2560], func=AF.Copy), [wr], [R("WP")])
            for e_ in range(e0, 6):
                bk = [0, 1, 6, 7, 2, 3, 4, 5][state["g"] % 8]
                state["g"] += 1
                pt, pr = psbank(bk)
                for kc in range(8):
                    sc.op("pe", lambda e, kc=kc, e_=e_, W=W, pt=pt: e.matmul(
                        pt[:, 0:256], lhsT=xT[:, kc, e_ * 128:(e_ + 1) * 128], rhs=W[:, kc * 256:(kc + 1) * 256],
                        start=(kc == 0), stop=(kc == 7)), [wr, R("xT")], [pr])
                sc.op("dve", lambda e, e_=e_, pt=pt: e.tensor_copy(
                    out=VA[:, e_, :, 0:64], in_=pt[:, 0:256].rearrange("p (j c) -> p j c", j=4)), [pr], [R("VA", e_)])
            if t == 0:
                bk = [0, 1, 6, 7, 2, 3, 4, 5][state["g"] % 8]
                state["g"] += 1
                pt, pr = psbank(bk)
                for kc in range(8):
                    sc.op("pe", lambda e, kc=kc, W=W, pt=pt: e.matmul(
                        pt[0:16, 0:256], lhsT=xT[:, kc, 128 + 112:256], rhs=W[:, kc * 256:(kc + 1) * 256],
                        start=(kc == 0), stop=(kc == 7)), [wr, R("xT")], [pr])
                sc.op("dve", lambda e, pt=pt: e.tensor_copy(
                    out=VAm[0:16, :, 0:64], in_=pt[0:16, 0:256].rearrange("p (j c) -> p j c", j=4)), [pr], [R("VAm")])
                sc.op("dve", lambda e: e.tensor_copy(out=KTm[:, :, :], in_=KT[:, :, 128 + 112:256]), [R("KT")], [R("KTm")])
            wprefetch()
            for qc in range(2):
                W, wr = wnext(l, 3 + qc)
                for mm in range(4):
                    m = qc * 4 + mm
                    bk = [0, 1, 6, 7, 2, 3, 4, 5][state["g"] % 8]
                    state["g"] += 1
                    pt, pr = psbank(bk)
                    for kc in range(8):
                        sc.op("pe", lambda e, kc=kc, mm=mm, W=W, pt=pt: e.matmul(
                            pt[:, :], lhsT=W[:, kc * 512 + mm * 128: kc * 512 + (mm + 1) * 128],
                            rhs=xT[:, kc, 128:640], start=(kc == 0), stop=(kc == 7)), [wr, R("xT")], [pr])
                    jq, hb = m // 2, 2 * (m % 2)
                    sc.op("act", lambda e, jq=jq, hb=hb, pt=pt: e.activation(
                        out=QTz[0:64, jq, :, hb * 128:(hb + 1) * 128], in_=pt[0:64, :].rearrange("p (i q) -> p i q", i=4),
                        func=AF.Copy), [pr], [R("QTz", jq)])
                    sc.op("act", lambda e, jq=jq, hb=hb, pt=pt: e.activation(
                        out=QTz[64:128, jq, :, (hb + 1) * 128:(hb + 2) * 128], in_=pt[64:128, :].rearrange("p (i q) -> p i q", i=4),
                        func=AF.Copy), [pr], [R("QTz", jq)])
                wprefetch()
            units = [(i, j) for i in range(NT) for j in range(4)]
            sctr = [0]

            def stage_a(u, i, j):
                pb = u % 3
                pos = p0 + i
                for c in range(4):
                    pa, pra = psbank(2 + sctr[0] % 4)
                    sctr[0] += 1
                    if c == 0:
                        nk = 16
                        lhs = KTm[:, j, :]
                        kres = R("KTm")
                    else:
                        nk = 128
                        e_ = i + c - 1
                        lhs = KT[:, j, e_ * 128:(e_ + 1) * 128]
                        kres = R("KT")
                    sc.op("pe", lambda e, pa=pa, lhs=lhs, nk=nk, c=c: e.matmul(
                        pa[0:nk, :], lhsT=lhs, rhs=QTz[:, j, i, :], start=True, stop=(c == 0)), [kres, R("QTz", j)], [pra])
                    if c == 0:
                        sc.op("act", lambda e, pa=pa: e.activation(
                            out=PT[pb][0:16, 0, :], in_=pa[0:16, :], func=AF.Exp, scale=0.125), [pra], [R("PT", pb, 0)])
                    else:
                        sc.op("pe", lambda e, pa=pa, c=c: e.matmul(
                            pa[:, :], lhsT=ident[:, :], rhs=alibi[:, c - 1, j, :], start=False, stop=True),
                            [R("ident"), R("alibi")], [pra])
                        sc.op("act", lambda e, pa=pa, c=c: e.activation(
                            out=PT[pb][:, c, :], in_=pa[:, :], func=AF.Exp, bias=pen[:, pos, c - 1:c], scale=0.125),
                            [pra, R("pen")], [R("PT", pb, c)])

            def stage_b(u, i, j):
                pb = u % 3
                rb = u % 2
                ib = i % 2
                po, pro = psbank(6 + u % 2)
                for blk in range(4):
                    for c in range(4):
                        if c == 0:
                            lhs = PT[pb][:, 0, blk * 128:(blk + 1) * 128]
                            rhs = VAm[:, j, 0:65]
                            rr = [R("VAm"), R("PT", pb), R("PT", pb, 0)]
                        else:
                            e_ = i + c - 1
                            lhs = PT[pb][:, c, blk * 128:(blk + 1) * 128]
                            rhs = VA[:, e_, j, 0:65]
                            rr = [R("VA", e_), R("VA"), R("PT", pb), R("PT", pb, c)]
                        sc.op("pe", lambda e, lhs=lhs, rhs=rhs, c=c, po=po, blk=blk: e.matmul(
                            po[:, blk * 65:(blk + 1) * 65], lhsT=lhs, rhs=rhs, start=(c == 0), stop=(c == 3)), rr, [pro])
                pov = po[:, 0:260].rearrange("p (b c) -> p b c", b=4)
                sc.op("dve", lambda e, pov=pov, rb=rb: e.tensor_tensor(
                    out=den[rb][:, :], in0=pov[:, :, 64], in1=esink[:, l * 16 + j * 4: l * 16 + j * 4 + 4], op=ALU.add),
                    [pro, R("esink")], [R("den", rb)])
                sc.op("dve", lambda e, rb=rb: e.reciprocal(out=rcp[rb][:, :], in_=den[rb][:, :]), [R("den", rb)], [R("rcp", rb)])
                for blk in range(4):
                    h = 4 * j + blk
                    sc.op("dve", lambda e, pov=pov, rb=rb, blk=blk, h=h, ib=ib: e.tensor_scalar(
                        out=ATtok[ib][:, h * 64:(h + 1) * 64], in0=pov[:, blk, 0:64], scalar1=rcp[rb][:, blk:blk + 1],
                        scalar2=None, op0=ALU.mult), [pro, R("rcp", rb)], [R("ATtok", ib)])
                if j == 3:
                    pending_tr.append([i, ib, 0])

            pending_tr = []

            def at_transposes(i, ib):
                if True:
                    pt, pr = psbank(state["g"] % 2)
                    state["g"] += 1
                    ptb = pt.bitcast(BF16)
                    for kc in range(8):
                        sc.op("pe", lambda e, kc=kc, ptb=ptb, ib=ib: e.transpose(
                            out=ptb[:, kc * 128:(kc + 1) * 128], in_=ATtok[ib][:, kc * 128:(kc + 1) * 128], identity=ident[:]),
                            [R("ATtok", ib), R("ident")], [pr])
                    sc.op("dve", lambda e, ptb=ptb: e.tensor_copy(
                        out=AT[:, :, i * 128:(i + 1) * 128], in_=ptb.rearrange("p (k t) -> p k t", k=8)),
                        [pr], [R("AT", k) for k in range(8)])

            def pool_blend(i):
                pos = p0 + i
                ab = i % 2
                if pos != 0:
                    sc.op("dve", lambda e, ab=ab, pos=pos: e.scalar_tensor_tensor(
                        out=Abl[ab][:, :, 120:128], in0=cp4(0, 32), scalar=nvp[:, pos:pos + 1], in1=cp4(32, 64),
                        op0=ALU.mult, op1=ALU.add), [R("cpool"), R("nvp")], [R("Abl", ab)])
                sc.op("dve", lambda e, ab=ab, pos=pos: e.tensor_scalar(
                    out=Anx[ab][:, :, :], in0=cp4(96, 128), scalar1=nvp[:, pos:pos + 1], scalar2=None, op0=ALU.mult),
                    [R("cpool"), R("nvp")], [R("Anx", ab)])

            def pool_slot(i):
                pos = p0 + i
                e_ = i + 1
                ab = i % 2
                if pos == 0:
                    aown = Ast
                    aown_r = R("Ast")
                else:
                    aown = Abl[ab]
                    aown_r = R("Abl", ab)
                pt, pr = psbank(state["g"] % 2)
                state["g"] += 1
                for g in range(4):
                    sc.op("pe", lambda e, g=g, aown=aown, pt=pt, e_=e_: e.matmul(
                        pt[:, g * 128:(g + 1) * 128], lhsT=U[:, e_, g * 128:(g + 1) * 128], rhs=aown[:, g, :],
                        start=True, stop=False), [R("U", e_), aown_r], [pr])
                    if pos != 0:
                        sc.op("pe", lambda e, g=g, pt=pt, e_=e_: e.matmul(
                            pt[:, g * 128:g * 128 + 8], lhsT=U[:, e_ - 1, g * 128:(g + 1) * 128], rhs=Apv[:, g, :],
                            start=False, stop=False), [R("U", e_ - 1), R("Apv")], [pr])
                    sc.op("pe", lambda e, g=g, pt=pt, e_=e_, ab=ab: e.matmul(
                        pt[:, g * 128 + 120:g * 128 + 128], lhsT=U[:, e_ + 1, g * 128:(g + 1) * 128], rhs=Anx[ab][:, g, :],
                        start=False, stop=True), [R("U", e_ + 1), R("Anx", ab)], [pr])
                sc.op("dve", lambda e, pt=pt, i=i: e.tensor_copy(
                    out=DT[:, :, i * 128:(i + 1) * 128], in_=pt[:, :].rearrange("p (g t) -> p g t", g=4)),
                    [pr], [R("DT")])
                if i + 1 < NT:
                    pool_blend(i + 1)

            wprefetch()
            if l == 0:
                convert(0, max(2, -(-NCH_M // ntiles)))
            else:
                convert(l + 1, 2)
            pool_blend(0)
            for u, (i, j) in enumerate(units):
                stage_a(u, i, j)
                for ent in pending_tr:
                    ent[2] += 1
                while pending_tr and pending_tr[0][2] >= 3:
                    ent = pending_tr.pop(0)
                    at_transposes(ent[0], ent[1])
                if u >= 2:
                    stage_b(u - 2, *units[u - 2])
                if u % 4 == 1:
                    pool_slot(u // 4)
            stage_b(len(units) - 2, *units[-2])
            stage_b(len(units) - 1, *units[-1])
            while pending_tr:
                ent = pending_tr.pop(0)
                at_transposes(ent[0], ent[1])
            for g in range(4):
                pt, pr = psbank(state["g"] % 2)
                state["g"] += 1
                sc.op("pe", lambda e, g=g, pt=pt: e.matmul(pt[:, :], lhsT=WP[:, g * 128:(g + 1) * 128], rhs=DT[:, g, :],
                                                           start=True, stop=True), [R("WP"), R("DT")], [pr])
                sc.op("act", lambda e, g=g, pt=pt: e.activation(out=YP[:, g, :], in_=pt[:, :], func=AF.Copy,
                                                                scale=pst[:, l * 4 + g: l * 4 + g + 1]),
                      [pr, R("pst")], [R("YP")])
            for m in range(8):
                W, wr = wnext(l, 5 + m)
                st = m % 2
                banks = [2, 3, 4, 5] if st == 0 else [6, 7, 0, 1]
                (pga, rga), (pya, rya), (pgb, rgb), (pyb, ryb) = [psbank(b) for b in banks]
                for kc in range(8):
                    sc.op("pe", lambda e, kc=kc, W=W, pga=pga: e.matmul(
                        pga[:, :], lhsT=W[:, kc * 128:(kc + 1) * 128], rhs=xT[:, kc, 128:640], start=(kc == 0), stop=(kc == 7)),
                        [wr, R("xT")], [rga])
                sc.op("act", lambda e, pga=pga, st=st: e.activation(out=TH[st][:, 0, :], in_=pga[:, :], func=AF.Tanh, scale=0.5),
                      [rga], [R("TH", st, 0)])
                for kc in range(8):
                    sc.op("pe", lambda e, kc=kc, W=W, pya=pya: e.matmul(
                        pya[:, :], lhsT=W[:, 2048 + kc * 128: 2048 + (kc + 1) * 128], rhs=AT[:, kc, :], start=(kc == 0), stop=(kc == 7)),
                        [wr, R("AT", kc)], [rya])
                sc.op("dve", lambda e, pya=pya, st=st: e.scalar_tensor_tensor(
                    out=TU[st][:], in0=TH[st][:, 0, :], scalar=1.0, in1=pya[:, :], op0=ALU.add, op1=ALU.mult),
                    [rya, R("TH", st, 0)], [R("TU", st)])
                for kc in range(8):
                    sc.op("pe", lambda e, kc=kc, W=W, pgb=pgb: e.matmul(
                        pgb[:, :], lhsT=W[:, 1024 + kc * 128: 1024 + (kc + 1) * 128], rhs=xT[:, kc, 128:640], start=(kc == 0), stop=(kc == 7)),
                        [wr, R("xT")], [rgb])
                sc.op("act", lambda e, pgb=pgb, st=st: e.activation(out=TH[st][:, 1, :], in_=pgb[:, :], func=AF.Tanh, scale=0.5),
                      [rgb], [R("TH", st, 1)])
                for kc in range(4):
                    sc.op("pe", lambda e, kc=kc, W=W, pyb=pyb: e.matmul(
                        pyb[:, :], lhsT=W[:, 3072 + kc * 128: 3072 + (kc + 1) * 128], rhs=YP[:, kc, :], start=(kc == 0), stop=(kc == 3)),
                        [wr, R("YP")], [ryb])
                sc.op("dve", lambda e, pyb=pyb, st=st: e.scalar_tensor_tensor(
                    out=TH[st][:, 1, :], in0=TH[st][:, 1, :], scalar=1.0, in1=pyb[:, :], op0=ALU.add, op1=ALU.mult),
                    [ryb, R("TH", st, 1)], [R("TH", st, 1)])
                sc.op("dve", lambda e, st=st, m=m: e.tensor_tensor(out=MT[:, m, :], in0=TU[st][:], in1=TH[st][:, 1, :], op=ALU.add),
                      [R("TU", st), R("TH", st, 1)], [R("MT", m)])
                wprefetch()

        def a_body2(l, t):
            p0 = t * NT
            for i in range(NT):
                pos = p0 + i
                sc.dma("pool", lambda e, pos=pos, i=i: e.dma_start(out=XA[:, i, :], in_=xs[pos * 128:(pos + 1) * 128, :]),
                       [R("xs", pos)], [R("XA", i)], f"xa{i}")
                sc.op("act", lambda e, i=i: e.activation(out=XA[:, i, :], in_=XA[:, i, :], func=AF.Copy, scale=ALPHA),
                      [R("XA", i)], [R("XA", i)])
            Ws = [wnext(l, 13), wnext(l, 14)]
            for i in range(NT):
                for hf in range(2):
                    W, wr = Ws[hf]
                    pt, pr = psbank([2, 3, 4, 5, 6, 7, 0, 1][state["g"] % 8])
                    state["g"] += 1
                    for kc in range(8):
                        sc.op("pe", lambda e, kc=kc, i=i, W=W, pt=pt: e.matmul(
                            pt[:, :], lhsT=MT[:, kc, i * 128:(i + 1) * 128], rhs=W[:, kc * 512:(kc + 1) * 512],
                            start=(kc == 0), stop=(kc == 7)), [wr, R("MT", kc)], [pr])
                    sc.op("dve", lambda e, i=i, hf=hf, pt=pt: e.scalar_tensor_tensor(
                        out=XA[:, i, hf * 512:(hf + 1) * 512], in0=pt[:, :], scalar=0.5,
                        in1=XA[:, i, hf * 512:(hf + 1) * 512], op0=ALU.mult, op1=ALU.add), [pr, R("XA", i)], [R("XA", i)])
            wprefetch()
            ln_group(NT, lambda i: store_rows(x1s, p0 + i, i, R("x1s", p0 + i)))

        load_consts()
        pass_emb()

        def head_ld(tk):
            (a_head_load if tk[0] == "A" else m_head_load)(tk[1], tk[2])

        def head_t(tk):
            (a_head_tr if tk[0] == "A" else m_head_tr)(tk[1], tk[2])

        def body1(tk):
            (a_body1 if tk[0] == "A" else m_body1)(tk[1], tk[2])

        def body2(tk):
            if tk[0] == "A":
                a_body2(tk[1], tk[2])
            else:
                m_body2(tk[1], tk[2], yout if tk[1] == L - 1 else xs)

        head_ld(tasks[0])
        head_t(tasks[0])
        for k, tk in enumerate(tasks):
            if k + 1 < len(tasks):
                head_ld(tasks[k + 1])
            if tk[0] == "M":
                convert(tk[1] + 1, -(-NCH // ntiles))
            body1(tk)
            if k + 1 < len(tasks):
                head_t(tasks[k + 1])
            body2(tk)
        sc.final_wait("sp")

        with nc.Block() as block:
            sc.emit(nc, block, stack)
    return nc, sc


def _kcview(w):
    K, C = w.shape
    return w.reshape(K // 128, 128, C).transpose(1, 0, 2)


def build_wst(w_in, w_pool, w_bo_attn, w_bo_pool, w_out, w_mlp1, w_mlp2):
    L = w_in.shape[0]
    out = np.zeros((L, NCH, 128, CH), np.float32)
    for l in range(L):
        wi = _kcview(w_in[l])
        q, k, v, u = wi[:, :, 0:1024], wi[:, :, 1024:1280], wi[:, :, 1280:1536], wi[:, :, 1536:2048]
        ga, gb = wi[:, :, 2048:3072], wi[:, :, 3072:4096]
        kd = np.concatenate([k[:, :, j * 64:(j + 1) * 64] for j in range(4) for _ in range(2)], axis=2)
        out[l, 0] = kd.reshape(128, CH)
        out[l, 1] = u.reshape(128, CH)
        out[l, 2, :, 0:2048] = v.reshape(128, 2048)
        out[l, 2, :, 2048:2560] = w_pool[l].transpose(1, 0, 2).reshape(128, 512)
        out[l, 3] = q[:, :, 0:512].reshape(128, CH)
        out[l, 4] = q[:, :, 512:1024].reshape(128, CH)
        boa, bop = _kcview(w_bo_attn[l]), _kcview(w_bo_pool[l])
        for m in range(8):
            sl = slice(m * 128, (m + 1) * 128)
            out[l, 5 + m, :, 0:1024] = ga[:, :, sl].reshape(128, 1024)
            out[l, 5 + m, :, 1024:2048] = gb[:, :, sl].reshape(128, 1024)
            out[l, 5 + m, :, 2048:3072] = boa[:, :, sl].reshape(128, 1024)
            out[l, 5 + m, :, 3072:3584] = bop[:, :, sl].reshape(128, 512)
        wo = _kcview(w_out[l])
        out[l, 13] = wo[:, :, 0:512].reshape(128, CH)
        out[l, 14] = wo[:, :, 512:1024].reshape(128, CH)
        w1 = _kcview(w_mlp1[l])
        for c in range(8):
            out[l, NCH_A + c] = w1[:, :, c * 512:(c + 1) * 512].reshape(128, CH)
        w2 = _kcview(w_mlp2[l])
        for hf in range(2):
            for kg in range(4):
                out[l, NCH_A + 8 + hf * 4 + kg] = w2[:, kg * 8:(kg + 1) * 8, hf * 512:(hf + 1) * 512].reshape(128, CH)
    return out


def build_static_consts():
    slopes = (2.0 ** (-8.0 * np.arange(1, NH + 1) / NH)).astype(np.float32)
    jj = np.arange(128)[:, None].astype(np.float32)
    ii = np.arange(128)[None, :].astype(np.float32)
    alibi = np.zeros((128, 3, 4, 4, 128), np.float32)
    for j in range(4):
        for blk in range(4):
            h = 4 * j + blk
            s = slopes[h]
            alibi[:, 0, j, blk] = np.where(jj >= ii, -s * (128.0 + ii - jj), NEG)
            alibi[:, 1, j, blk] = -s * np.abs(jj - ii)
            alibi[:, 2, j, blk] = np.where(jj <= ii, -s * (128.0 + jj - ii), NEG)
    alibi = (8.0 * alibi).reshape(128, 3 * 4 * 512)

    def band(lo_clip=None, hi_clip=None):
        A = np.zeros((128, 4, 128), np.float64)
        for g, w in enumerate(POOL_W):
            for t in range(128):
                lo, hi = t - w // 2, t + w // 2
                lo_e = lo if lo_clip is None else max(lo, lo_clip)
                hi_e = hi if hi_clip is None else min(hi, hi_clip)
                if lo_clip is not None and t < lo_clip:
                    lo_e, hi_e = lo, hi
                cnt = hi_e - lo_e
                for s in range(max(lo_e, 0), min(hi_e, 128)):
                    A[s, g, t] += 1.0 / cnt
                A[t, g, t] -= 1.0
        return A

    A_norm = band()
    A_end = band(hi_clip=128)
    A_start = band(lo_clip=112)
    A_prev8 = np.zeros((128, 4, 8), np.float64)
    A_next8 = np.zeros((128, 4, 8), np.float64)
    for g, w in enumerate(POOL_W):
        for t in range(8):
            for sp_ in range(128):
                if sp_ - 128 >= t - w // 2:
                    A_prev8[sp_, g, t] = 1.0 / w
        for tt in range(8):
            t = 120 + tt
            for sp_ in range(128):
                if 128 + sp_ <= t + w // 2 - 1:
                    A_next8[sp_, g, tt] = 1.0 / w
    cpool = np.zeros((128, 1152), np.float32)
    cpool[:, 0:512] = A_norm.reshape(128, 512)
    cpool[:, 512:1024] = A_start.reshape(128, 512)
    cpool[:, 1024:1056] = (A_norm[:, :, 120:128] - A_end[:, :, 120:128]).reshape(128, 32)
    cpool[:, 1056:1088] = A_end[:, :, 120:128].reshape(128, 32)
    cpool[:, 1088:1120] = A_prev8.reshape(128, 32)
    cpool[:, 1120:1152] = A_next8.reshape(128, 32)
    return alibi, cpool


def chain_tables(blocks, ndummy, S):
    assert 1 + len(blocks) + ndummy == S
    pv = np.zeros(S, np.float32)
    ov = np.ones(S, np.float32)
    nv = np.zeros(S, np.float32)
    ov[0] = 0.0
    nv[0] = 1.0 if (len(blocks) and blocks[0] == 0) else 0.0
    for p in range(1, 1 + len(blocks)):
        b = blocks[p - 1]
        if p >= 2 and blocks[p - 2] == b - 1:
            pv[p] = 1.0
        if p < len(blocks) and blocks[p] == b + 1:
            nv[p] = 1.0
    pen = np.stack([pv, ov, nv], axis=1)
    pen = (pen - 1.0) * (-NEG)
    pen_t = np.broadcast_to(pen.reshape(1, S * 3), (128, S * 3)).astype(np.float32).copy()
    nvp_t = np.broadcast_to(nv.reshape(1, S), (128, S)).astype(np.float32).copy()
    return pen_t, nvp_t


def build_xin(x_seq, meta_tokens, blocks, ndummy, S):
    xin = np.zeros((S * 128, D), np.float32)
    xin[112:128] = meta_tokens
    for p, b in enumerate(blocks):
        xin[(p + 1) * 128:(p + 2) * 128] = x_seq[b * 128:(b + 1) * 128]
    return xin


def build_vec_consts(L, ln_emb_g, ln_emb_b, sink, pool_scale, ln1_g, ln1_b, b_mlp1, b_mlp2, ln2_g, ln2_b):
    c_vec = np.zeros((5 * L + 2, D), np.float32)
    for l in range(L):
        c_vec[5 * l + 0] = ln1_g[l]
        c_vec[5 * l + 1] = ln1_b[l]
        c_vec[5 * l + 2] = ln2_g[l]
        c_vec[5 * l + 3] = ln2_b[l]
        c_vec[5 * l + 4] = b_mlp2[l]
    c_vec[5 * L] = ln_emb_g
    c_vec[5 * L + 1] = ln_emb_b
    c_b1 = np.ascontiguousarray(b_mlp1[:L].reshape(L, 32, 128).transpose(2, 0, 1).reshape(128, L * 32))
    c_ps = np.ascontiguousarray(pool_scale[:L].reshape(L, 4, 128).transpose(2, 0, 1).reshape(128, L * 4))
    perm = np.arange(16)
    c_sink = np.ascontiguousarray(sink[:L][:, perm].reshape(1, L * 16))
    return c_vec, c_b1, c_ps, c_sink


S_FULL = 40
_PROG_CACHE = {}


def core_plan():
    plan = []
    for b in range(2):
        plan.append(("p", b, list(range(0, 36)), 3, list(range(0, 32))))
        plan.append(("p", b, [0, 1, 2] + list(range(28, 64)), 0, list(range(32, 64))))
    for b in range(4):
        plan.append(("s", b, list(range(32)), 7, list(range(32))))
    return plan


def kernel(x_prompt, x_sample, meta_tokens, ln_emb_g, ln_emb_b, w_in, sink, w_pool, pool_scale,
           w_bo_attn, w_bo_pool, w_out, ln1_g, ln1_b, w_mlp1, b_mlp1, w_mlp2, b_mlp2, ln2_g, ln2_b):
    f = lambda a: np.asarray(a, dtype=np.float32)
    x_prompt, x_sample, meta_tokens = f(x_prompt), f(x_sample), f(meta_tokens)
    L = DEPTH
    S = S_FULL
    if "nc" not in _PROG_CACHE:
        _PROG_CACHE["nc"] = build_program(S, L)[0]
    nc = _PROG_CACHE["nc"]
    wst = build_wst(f(w_in), f(w_pool), f(w_bo_attn), f(w_bo_pool), f(w_out), f(w_mlp1), f(w_mlp2))
    alibi, cpool = build_static_consts()
    c_vec, c_b1, c_ps, c_sink = build_vec_consts(L, f(ln_emb_g), f(ln_emb_b), f(sink), f(pool_scale), f(ln1_g), f(ln1_b),
                                                 f(b_mlp1), f(b_mlp2), f(ln2_g), f(ln2_b))
    plan = core_plan()
    in_maps = []
    for (src, b, blocks, ndummy, own) in plan:
        xseq = x_prompt[b] if src == "p" else x_sample[b]
        pen_t, nvp_t = chain_tables(blocks, ndummy, S)
        in_maps.append({
            "xin": build_xin(xseq, meta_tokens, blocks, ndummy, S), "wst": wst, "c_alibi": alibi, "c_pen": pen_t,
            "c_nvp": nvp_t, "c_pool": cpool, "c_vec": c_vec, "c_b1": c_b1, "c_ps": c_ps, "c_sink": c_sink,
        })
    res = run_bass_kernel_spmd(nc, in_maps, core_ids=list(range(8)))
    y_prompt = np.zeros(x_prompt.shape, np.float32)
    y_sample = np.zeros(x_sample.shape, np.float32)
    for ci, (src, b, blocks, ndummy, own) in enumerate(plan):
        y = res.results[ci]["y"]
        dst = y_prompt if src == "p" else y_sample
        for p, blk in enumerate(blocks):
            if blk in own:
                dst[b, blk * 128:(blk + 1) * 128] = y[(p + 1) * 128:(p + 2) * 128]
    return (y_prompt, y_sample)
```

```python
import contextlib
import numpy as np
import concourse.bass as bass
import concourse.mybir as mybir
from concourse.bass_utils import run_bass_kernel_spmd

F32 = mybir.dt.float32
BF16 = mybir.dt.bfloat16
ALU = mybir.AluOpType
AF = mybir.ActivationFunctionType

D = 1024
DEPTH = 4
NH = 16
NKV = 4
N_META = 16
ALPHA = float((2 * DEPTH) ** 0.25)
LN_EPS = 1e-5
NEG = -30000.0
POOL_W = (2, 4, 8, 16)
NCH_A = 15
NCH_M = 16
NCH = NCH_A + NCH_M
CH = 4096
ENGS = ("pe", "act", "dve", "pool", "sp")
EPOCH = 16000
import os
USE_FP32_STREAM = bool(os.environ.get("FP32STREAM"))


class Sched:
    def __init__(self):
        self.ops = {e: [] for e in ENGS}
        self.cnt = {e: 0 for e in ENGS}
        self.known = {e: {} for e in ENGS}
        self.lastw = {}
        self.readers = {}
        self.dmacnt = {}
        self.snap = {}
        self.nwaits = 0

    def _deps(self, eng, reads, writes, is_dma):
        deps = {}

        def add(p, kind):
            if (not is_dma) and p[0] == "e" and p[1] == eng and kind != "raw":
                return
            k = (p[0], p[1])
            if deps.get(k, 0) < p[2]:
                deps[k] = p[2]

        for r in reads:
            w = self.lastw.get(r)
            if w is not None:
                add(w, "raw")
        for w_ in writes:
            w = self.lastw.get(w_)
            if w is not None:
                add(w, "waw")
            for rd in self.readers.get(w_, ()):
                add(rd, "war")
        return deps

    def _filter(self, eng, deps):
        kn = self.known[eng]
        waits = []
        for k, v in deps.items():
            if kn.get(k, 0) >= v:
                continue
            waits.append((k, v))
        for k, v in waits:
            if kn.get(k, 0) < v:
                kn[k] = v
            sn = self.snap.get((k[0], k[1], v))
            if sn:
                for k2, v2 in sn.items():
                    if kn.get(k2, 0) < v2:
                        kn[k2] = v2
        self.nwaits += len(waits)
        return waits

    def _update(self, me, reads, writes):
        for r in reads:
            self.readers.setdefault(r, []).append(me)
        for w_ in writes:
            self.lastw[w_] = me
            self.readers[w_] = []

    def op(self, eng, fn, reads=(), writes=()):
        deps = self._deps(eng, reads, writes, False)
        waits = self._filter(eng, deps)
        self.cnt[eng] += 1
        seq = self.cnt[eng]
        self.ops[eng].append(("op", waits, fn, seq))
        self.snap[("e", eng, seq)] = dict(self.known[eng])
        self._update(("e", eng, seq), reads, writes)

    def dma(self, q, fn, reads, writes, key, serialize=False):
        deps = self._deps(q, reads, writes, True)
        if serialize and self.dmacnt.get(key, 0) > 0:
            deps[("d", key)] = max(deps.get(("d", key), 0), self.dmacnt[key])
        waits = self._filter(q, deps)
        v = self.dmacnt.get(key, 0) + 16
        self.dmacnt[key] = v
        self.ops[q].append(("dma", waits, fn, key))
        self.snap[("d", key, v)] = dict(self.known[q])
        self._update(("d", key, v), reads, writes)

    def final_wait(self, eng):
        waits = []
        for key, v in self.dmacnt.items():
            if self.known[eng].get(("d", key), 0) < v:
                waits.append((("d", key), v))
        self.ops[eng].append(("wait", waits, None, None))

    def emit(self, nc, block, stack):
        esem = {}
        for e in ENGS:
            n = (self.cnt[e] + EPOCH - 1) // EPOCH
            esem[e] = [stack.enter_context(nc.semaphore(f"s_{e}_{i}")) for i in range(max(n, 1))]
        dsem = {}
        for i, key in enumerate(self.dmacnt):
            dsem[key] = stack.enter_context(nc.semaphore(f"d_{i}"))

        def do_waits(eng, waits):
            for (k, v) in waits:
                if k[0] == "e":
                    eng.wait_ge(esem[k[1]][(v - 1) // EPOCH], (v - 1) % EPOCH + 1)
                else:
                    eng.wait_ge(dsem[k[1]], v)

        def run(ename):
            def body(eng):
                for rec in self.ops[ename]:
                    kind, waits, fn, extra = rec
                    do_waits(eng, waits)
                    if kind == "op":
                        inst = fn(eng)
                        inst.then_inc(esem[ename][(extra - 1) // EPOCH], 1)
                    elif kind == "dma":
                        inst = fn(eng)
                        inst.then_inc(dsem[extra], 16)
            return body

        block.tensor(run("pe"))
        block.scalar(run("act"))
        block.vector(run("dve"))
        block.gpsimd(run("pool"))
        block.sync(run("sp"))


def build_program(S, L, debug_layers=None):
    NT = 4
    assert S % NT == 0
    ntiles = S // NT
    nc = bass.Bass("TRN2", target_bir_lowering=False)
    sc = Sched()

    def din(name, shape, dt=F32):
        return nc.dram_tensor(name, shape, dt, kind="ExternalInput").ap()

    xin = din("xin", [S * 128, D])
    wst = din("wst", [L, NCH, 128, CH])
    c_alibi = din("c_alibi", [128, 3 * 4 * 512])
    c_pen = din("c_pen", [128, S * 3])
    c_nvp = din("c_nvp", [128, S])
    c_pool = din("c_pool", [128, 1152])
    c_vec = din("c_vec", [5 * L + 2, D])
    c_b1 = din("c_b1", [128, L * 32])
    c_ps = din("c_ps", [128, L * 4])
    c_sink = din("c_sink", [1, L * 16])
    yout = nc.dram_tensor("y", [S * 128, D], F32, kind="ExternalOutput").ap()
    xs = nc.dram_tensor("xs", [S * 128, D], F32, kind="Internal").ap()
    x1s = nc.dram_tensor("x1s", [S * 128, D], F32, kind="Internal").ap()
    wbf = nc.dram_tensor("wbf", [L, NCH, 128, CH], BF16, kind="Internal").ap()

    stack = contextlib.ExitStack()
    with stack:
        def sb(name, shape, dt):
            return stack.enter_context(nc.sbuf_tensor(name, shape, dt))

        NRING = 5
        ring = [sb(f"ring{i}", [128, CH], BF16) for i in range(NRING)]
        xb = [sb(f"xb{i}", [128, D], BF16) for i in range(6)]
        XA = sb("XA", [128, 4, D], F32)
        xT = sb("xT", [128, 8, 768], BF16)
        arB = sb("arB", [128, 18432], BF16)
        KT = arB[:, 0:3072].rearrange("p (j t) -> p j t", j=4)
        QTz = arB[:, 3072:11264].rearrange("p (j i c) -> p j i c", j=4, i=4)
        U = arB[:, 11264:14336].rearrange("p (e c) -> p e c", e=6)
        MT = arB[:, 14336:18432].rearrange("p (m t) -> p m t", m=8)
        HT = arB[:, 0:16384].rearrange("p (m t) -> p m t", m=32)
        VA = sb("VA", [128, 6, 4, 128], BF16)
        PT = [sb(f"PT{i}", [128, 4, 512], BF16) for i in range(3)]
        AT = sb("AT", [128, 8, 512], BF16)
        DT = sb("DT", [128, 4, 512], BF16)
        YP = sb("YP", [128, 4, 512], BF16)
        WP = sb("WP", [128, 512], BF16)
        KTm = sb("KTm", [128, 4, 16], BF16)
        VAm = sb("VAm", [128, 4, 128], BF16)
        Abl = [sb(f"Abl{i}", [128, 4, 128], BF16) for i in range(2)]
        Anx = [sb(f"Anx{i}", [128, 4, 8], BF16) for i in range(2)]
        Ast = sb("Ast", [128, 4, 128], BF16)
        Apv = sb("Apv", [128, 4, 8], BF16)
        ident = sb("ident", [128, 128], BF16)
        NTF = 3
        tmpF = [sb(f"tmpF{i}", [128, 512], F32) for i in range(NTF)]
        den = [sb(f"den{i}", [128, 4], F32) for i in range(2)]
        rcp = [sb(f"rcp{i}", [128, 4], F32) for i in range(2)]
        ATtok = [sb(f"ATtok{i}", [128, D], BF16) for i in range(2)]
        esink = sb("esink", [128, L * 16], F32)
        TH = [sb(f"TH{i}", [128, 2, 512], F32) for i in range(2)]
        TU = [sb(f"TU{i}", [128, 512], F32) for i in range(2)]
        B2t = TH[0][:, :, :].rearrange("p a b -> p (a b)")
        alibi = sb("alibi", [128, 3, 4, 512], BF16)
        pen = sb("pen", [128, S, 3], F32)
        nvp = sb("nvp", [128, S], F32)
        cpool = sb("cpool", [128, 128], F32)
        Gt = sb("Gt", [128, D], F32)
        Bt = sb("Bt", [128, D], F32)
        b1t = sb("b1t", [128, L * 32], F32)
        pst = sb("pst", [128, L * 4], F32)
        stats = sb("stats", [128, 8, 2, 6], F32)
        mv = sb("mv", [128, 8, 2], F32)
        sd = sb("sd", [128, 8], F32)
        rstd = sb("rstd", [128, 8], F32)
        nmr = sb("nmr", [128, 8], F32)
        psall = stack.enter_context(nc.psum_tensor("psall", [128, 4096], F32))
        ps = [psall[:, k * 512:(k + 1) * 512] for k in range(8)]

        state = {"xst": 0, "xb": 0, "g": 0, "tf": 0}

        def R(*a):
            return tuple(a)

        def cp4(a, b):
            return cpool[:, a:b].rearrange("p (g t) -> p g t", g=4)

        def load_consts():
            sc.dma("pool", lambda e: e.dma_start(out=alibi[:].rearrange("p a b c -> p (a b c)"), in_=c_alibi[:, :]),
                   [], [R("alibi")], "c0")
            sc.dma("sp", lambda e: e.dma_start(out=pen[:].rearrange("p a b -> p (a b)"), in_=c_pen[:, :]),
                   [], [R("pen")], "c1")
            sc.dma("sp", lambda e: e.dma_start(out=nvp[:], in_=c_nvp[:, :]), [], [R("nvp")], "c2")
            sc.dma("sp", lambda e: e.dma_start(out=cpool[:], in_=c_pool[:, 1024:1152]), [], [R("cpool")], "c3")
            sc.dma("sp", lambda e: e.dma_start(out=b1t[:], in_=c_b1[:, :]), [], [R("b1t")], "c4")
            sc.dma("sp", lambda e: e.dma_start(out=pst[:], in_=c_ps[:, :]), [], [R("pst")], "c5")
            sc.dma("sp", lambda e: e.dma_start(out=esink[:], in_=c_sink[0:1, :].partition_broadcast(128)), [], [R("esink")], "c6")
            sc.op("act", lambda e: e.activation(out=esink[:], in_=esink[:], func=AF.Exp), [R("esink")], [R("esink")])
            sc.op("pool", lambda e: e.memset(ident[:], 0.0), [], [R("ident")])
            sc.op("pool", lambda e: e.affine_select(out=ident[:], in_=ident[:], pattern=[[-1, 128]],
                                                    compare_op=ALU.not_equal, fill=1.0, base=0,
                                                    channel_multiplier=1), [R("ident")], [R("ident")])
            for b in range(3):
                sc.op("pool", lambda e, b=b: e.memset(PT[b][:, 0, :], 0.0), [], [R("PT", b)])
            sc.op("pool", lambda e: e.memset(VAm[:], 0.0), [], [R("VAm")])
            sc.op("pool", lambda e: e.memset(VAm[0:16, :, 64:128], 1.0), [R("VAm")], [R("VAm")])
            sc.op("pool", lambda e: e.memset(VA[:, :, :, 64:128], 1.0), [], [R("VA")])
            for b in range(2):
                sc.dma("pool", lambda e, b=b: e.dma_start(out=Abl[b][:].rearrange("p g t -> p (g t)"), in_=c_pool[:, 0:512]),
                       [], [R("Abl", b)], f"cA{b}")
            sc.dma("pool", lambda e: e.dma_start(out=Ast[:].rearrange("p g t -> p (g t)"), in_=c_pool[:, 512:1024]),
                   [], [R("Ast")], "cA2")
            sc.op("dve", lambda e: e.tensor_copy(out=Apv[:], in_=cp4(64, 96)), [R("cpool")], [R("Apv")])

        def load_vec(tile, row, tag):
            sc.dma("pool", lambda e: e.dma_start(out=tile[:], in_=c_vec[row:row + 1, :].partition_broadcast(128)),
                   [], [R(tag)], "v_" + tag)

        tasks = []
        for l in range(L):
            tasks += [("A", l, t) for t in range(ntiles)]
            tasks += [("M", l, t) for t in range(ntiles)]
        wsched = []
        for (k, l, t) in tasks:
            wsched += [(l, c) for c in (range(0, NCH_A) if k == "A" else range(NCH_A, NCH))]
        wst_state = {"issued": 0, "taken": 0}

        def _wissue():
            k = wst_state["issued"]
            if k >= len(wsched):
                return
            l, c = wsched[k]
            slot = k % NRING
            t = ring[slot]
            if l == 0 and c < NCH_A:
                sc.dma("pool", lambda e: e.dma_start(out=t[:], in_=wst[l, c, :, :]), [], [R("ring", slot)], f"w{slot}")
            else:
                sc.dma("sp", lambda e: e.dma_start(out=t[:], in_=wbf[l, c, :, :]), [R("wbf", l, c)], [R("ring", slot)], f"w{slot}")
            wst_state["issued"] += 1

        cv_state = {"n": 0, "next": {0: NCH_A}}

        def convert(l, n):
            if l >= L:
                return
            c0 = cv_state["next"].get(l, 0)
            for c in range(c0, min(c0 + n, NCH)):
                key = f"cv{cv_state['n'] % 8}"
                cv_state["n"] += 1
                sc.dma("pool", lambda e, c=c: e.dma_start(
                    out=wbf[l, c, :, :].rearrange("(a b) f -> a (b f)", a=16),
                    in_=wst[l, c, :, :].rearrange("(a b) f -> a (b f)", a=16)), [], [R("wbf", l, c)], key, serialize=True)
            cv_state["next"][l] = min(c0 + n, NCH)

        def wnext(l, c):
            k = wst_state["taken"]
            assert wsched[k] == (l, c), (wsched[k], l, c)
            while wst_state["issued"] <= k:
                _wissue()
            wst_state["taken"] += 1
            return ring[k % NRING], R("ring", k % NRING)

        def wprefetch():
            while wst_state["issued"] < wst_state["taken"] + NRING:
                if wst_state["issued"] >= len(wsched):
                    break
                _wissue()

        def psbank(k):
            return ps[k], R("ps", k)

        def head_load(src, positions, res_name, k0=0):
            for k, pos in zip(range(k0, k0 + len(positions)), positions):
                sc.dma("pool", lambda e, k=k, pos=pos: e.dma_start(out=xb[k][:], in_=src[pos * 128:(pos + 1) * 128, :]),
                       [R(res_name, pos)], [R("xb", k)], f"xb{k}")

        def head_tr(n, k0=0):
            for k in range(k0, n):
                pt, pr = psbank(2 + state["g"] % 4)
                state["g"] += 1
                ptb = pt.bitcast(BF16)
                for kc in range(8):
                    sc.op("pe", lambda e, kc=kc, k=k, ptb=ptb: e.transpose(out=ptb[:, kc * 128:(kc + 1) * 128],
                                                                          in_=xb[k][:, kc * 128:(kc + 1) * 128], identity=ident[:]),
                          [R("xb", k), R("ident")], [pr])
                sc.op("dve", lambda e, k=k, ptb=ptb: e.tensor_copy(out=xT[:, :, k * 128:(k + 1) * 128],
                                                                   in_=ptb.rearrange("p (k t) -> p k t", k=8)),
                      [pr], [R("xT")])

        def ln_slot(i, after_slot):
            for h in range(2):
                sc.op("dve", lambda e, h=h: e.bn_stats(out=stats[:, i, h, :], in_=XA[:, i, h * 512:(h + 1) * 512]),
                      [R("XA", i)], [R("stats", i)])
            sc.op("dve", lambda e: e.bn_aggr(out=mv[:, i, :], in_=stats[:, i, :, :].rearrange("p a b -> p (a b)")),
                  [R("stats", i)], [R("mv", i)])
            sc.op("act", lambda e: e.activation(out=sd[:, i:i + 1], in_=mv[:, i, 1:2], func=AF.Sqrt, bias=LN_EPS, scale=1.0),
                  [R("mv", i)], [R("sd", i)])
            sc.op("dve", lambda e: e.reciprocal(out=rstd[:, i:i + 1], in_=sd[:, i:i + 1]), [R("sd", i)], [R("rstd", i)])
            sc.op("dve", lambda e: e.scalar_tensor_tensor(out=nmr[:, i:i + 1], in0=mv[:, i, 0:1], scalar=-1.0,
                                                          in1=rstd[:, i:i + 1], op0=ALU.mult, op1=ALU.mult),
                  [R("mv", i), R("rstd", i)], [R("nmr", i)])
            sc.op("act", lambda e: e.activation(out=XA[:, i, :], in_=XA[:, i, :], func=AF.Identity,
                                                scale=rstd[:, i:i + 1], bias=nmr[:, i:i + 1]),
                  [R("XA", i), R("rstd", i), R("nmr", i)], [R("XA", i)])
            sc.op("dve", lambda e: e.tensor_tensor(out=XA[:, i, :], in0=XA[:, i, :], in1=Gt[:], op=ALU.mult),
                  [R("XA", i), R("Gt")], [R("XA", i)])
            sc.op("pool", lambda e: e.tensor_tensor(out=XA[:, i, :], in0=XA[:, i, :], in1=Bt[:], op=ALU.add),
                  [R("XA", i), R("Bt")], [R("XA", i)])
            after_slot(i)

        def ln_group(n, after_slot):
            for i in range(n):
                for h in range(2):
                    sc.op("dve", lambda e, i=i, h=h: e.bn_stats(out=stats[:, i, h, :], in_=XA[:, i, h * 512:(h + 1) * 512]),
                          [R("XA", i)], [R("stats", i)])
                sc.op("dve", lambda e, i=i: e.bn_aggr(out=mv[:, i, :], in_=stats[:, i, :, :].rearrange("p a b -> p (a b)")),
                      [R("stats", i)], [R("mv", i)])
            for i in range(n):
                sc.op("act", lambda e, i=i: e.activation(out=sd[:, i:i + 1], in_=mv[:, i, 1:2], func=AF.Sqrt, bias=LN_EPS, scale=1.0),
                      [R("mv", i)], [R("sd", i)])
            for i in range(n):
                sc.op("dve", lambda e, i=i: e.reciprocal(out=rstd[:, i:i + 1], in_=sd[:, i:i + 1]), [R("sd", i)], [R("rstd", i)])
                sc.op("dve", lambda e, i=i: e.scalar_tensor_tensor(out=nmr[:, i:i + 1], in0=mv[:, i, 0:1], scalar=-1.0,
                                                                   in1=rstd[:, i:i + 1], op0=ALU.mult, op1=ALU.mult),
                      [R("mv", i), R("rstd", i)], [R("nmr", i)])
            for i in range(n):
                sc.op("act", lambda e, i=i: e.activation(out=XA[:, i, :], in_=XA[:, i, :], func=AF.Identity,
                                                         scale=rstd[:, i:i + 1], bias=nmr[:, i:i + 1]),
                      [R("XA", i), R("rstd", i), R("nmr", i)], [R("XA", i)])
            for i in range(n):
                sc.op("dve", lambda e, i=i: e.tensor_tensor(out=XA[:, i, :], in0=XA[:, i, :], in1=Gt[:], op=ALU.mult),
                      [R("XA", i), R("Gt")], [R("XA", i)])
            for i in range(n):
                sc.op("pool", lambda e, i=i: e.tensor_tensor(out=XA[:, i, :], in0=XA[:, i, :], in1=Bt[:], op=ALU.add),
                      [R("XA", i), R("Bt")], [R("XA", i)])
                after_slot(i)

        def store_rows(dst, pos, i, dst_res):
            sc.dma("pool", lambda e: e.dma_start(out=dst[pos * 128:(pos + 1) * 128, :], in_=XA[:, i, :]),
                   [R("XA", i)], [dst_res], f"st{i}")

        def pass_emb():
            load_vec(Gt, 5 * L, "Gt")
            load_vec(Bt, 5 * L + 1, "Bt")
            arF = arB[:, :].bitcast(F32)
            bufs = [arF[:, k * D:(k + 1) * D] for k in range(8)]
            for t in range(ntiles):
                ks = [(t % 2) * 4 + i for i in range(NT)]
                poss = [t * NT + i for i in range(NT)]
                for k, pos in zip(ks, poss):
                    sc.dma("sp", lambda e, k=k, pos=pos: e.dma_start(out=bufs[k], in_=xin[pos * 128:(pos + 1) * 128, :]),
                           [], [R("p0", k)], f"p0l{k}")
                for k in ks:
                    for h in range(2):
                        sc.op("dve", lambda e, k=k, h=h: e.bn_stats(out=stats[:, k, h, :], in_=bufs[k][:, h * 512:(h + 1) * 512]),
                              [R("p0", k)], [R("stats", k)])
                    sc.op("dve", lambda e, k=k: e.bn_aggr(out=mv[:, k, :], in_=stats[:, k, :, :].rearrange("p a b -> p (a b)")),
                          [R("stats", k)], [R("mv", k)])
                for k in ks:
                    sc.op("act", lambda e, k=k: e.activation(out=sd[:, k:k + 1], in_=mv[:, k, 1:2], func=AF.Sqrt, bias=LN_EPS, scale=1.0),
                          [R("mv", k)], [R("sd", k)])
                for k in ks:
                    sc.op("dve", lambda e, k=k: e.reciprocal(out=rstd[:, k:k + 1], in_=sd[:, k:k + 1]), [R("sd", k)], [R("rstd", k)])
                    sc.op("dve", lambda e, k=k: e.scalar_tensor_tensor(out=nmr[:, k:k + 1], in0=mv[:, k, 0:1], scalar=-1.0,
                                                                       in1=rstd[:, k:k + 1], op0=ALU.mult, op1=ALU.mult),
                          [R("mv", k), R("rstd", k)], [R("nmr", k)])
                for k in ks:
                    sc.op("act", lambda e, k=k: e.activation(out=bufs[k], in_=bufs[k], func=AF.Identity,
                                                             scale=rstd[:, k:k + 1], bias=nmr[:, k:k + 1]),
                          [R("p0", k), R("rstd", k), R("nmr", k)], [R("p0", k)])
                for k in ks:
                    sc.op("dve", lambda e, k=k: e.tensor_tensor(out=bufs[k], in0=bufs[k], in1=Gt[:], op=ALU.mult),
                          [R("p0", k), R("Gt")], [R("p0", k)])
                for k, pos in zip(ks, poss):
                    sc.op("pool", lambda e, k=k: e.tensor_tensor(out=bufs[k], in0=bufs[k], in1=Bt[:], op=ALU.add),
                          [R("p0", k), R("Bt")], [R("p0", k)])
                    sc.dma("pool", lambda e, k=k, pos=pos: e.dma_start(out=xs[pos * 128:(pos + 1) * 128, :], in_=bufs[k]),
                           [R("p0", k)], [R("xs", pos)], f"p0s{k}")
            allp0 = [R("p0", k) for k in range(8)]
            sc.op("dve", lambda e: e.memset(stats[:, 0, 0, 0:1], 0.0), [], allp0 + [R("stats", 0)])
            sc.op("act", lambda e: e.activation(out=sd[:, 0:1], in_=sd[:, 0:1], func=AF.Copy), [], allp0 + [R("sd", 0)])

        def m_head_load(l, t):
            head_load(x1s, [t * NT + i for i in range(NT)], "x1s")

        def m_head_tr(l, t):
            head_tr(NT)

        def m_body1(l, t):
            p0 = t * NT
            if t == 0:
                load_vec(Gt, 5 * l + 2, "Gt")
                load_vec(Bt, 5 * l + 3, "Bt")
                sc.dma("pool", lambda e: e.dma_start(out=B2t, in_=c_vec[5 * l + 4:5 * l + 5, :].partition_broadcast(128)),
                       [], [R("B2t"), R("TH", 0, 0), R("TH", 0, 1)], "v_B2t")
            for c in range(8):
                W, wr = wnext(l, NCH_A + c)
                for mm in range(4):
                    m = 4 * c + mm
                    bk = [0, 1, 6, 7, 2, 3, 4, 5][state["g"] % 8]
                    state["g"] += 1
                    pt, pr = psbank(bk)
                    for kc in range(8):
                        sc.op("pe", lambda e, kc=kc, mm=mm, W=W, pt=pt: e.matmul(
                            pt[:, :], lhsT=W[:, kc * 512 + mm * 128: kc * 512 + (mm + 1) * 128],
                            rhs=xT[:, kc, 0:512], start=(kc == 0), stop=(kc == 7)),
                            [wr, R("xT")], [pr])
                    tb = state["tf"] % NTF
                    state["tf"] += 1
                    sc.op("act", lambda e, pt=pt, tb=tb, m=m: e.activation(
                        out=tmpF[tb][:], in_=pt[:, :], func=AF.Relu, bias=b1t[:, l * 32 + m: l * 32 + m + 1], scale=1.0),
                        [pr, R("b1t")], [R("tmpF", tb)])
                    sc.op("dve", lambda e, tb=tb, m=m: e.tensor_tensor(out=HT[:, m, :], in0=tmpF[tb][:], in1=tmpF[tb][:], op=ALU.mult),
                          [R("tmpF", tb)], [R("HT", m)])
                wprefetch()

        def m_body2(l, t, dst):
            p0 = t * NT
            for i in range(NT):
                pos = p0 + i
                sc.dma("pool", lambda e, pos=pos, i=i: e.dma_start(out=XA[:, i, :], in_=x1s[pos * 128:(pos + 1) * 128, :]),
                       [R("x1s", pos)], [R("XA", i)], f"xa{i}")
                sc.op("act", lambda e, i=i: e.activation(out=XA[:, i, :], in_=XA[:, i, :], func=AF.Copy, scale=ALPHA),
                      [R("XA", i)], [R("XA", i)])
                sc.op("pool", lambda e, i=i: e.tensor_tensor(out=XA[:, i, :], in0=XA[:, i, :], in1=B2t, op=ALU.add),
                      [R("XA", i), R("B2t"), R("TH", 0, 0), R("TH", 0, 1)], [R("XA", i)])
            for hf in range(2):
                for kg in range(4):
                    W, wr = wnext(l, NCH_A + 8 + hf * 4 + kg)
                    for i in range(NT):
                        pt, pr = psbank(2 + i)
                        for kc in range(8):
                            kk = kg * 8 + kc
                            sc.op("pe", lambda e, kc=kc, kk=kk, i=i, W=W, pt=pt, kg=kg: e.matmul(
                                pt[:, :], lhsT=HT[:, kk, i * 128:(i + 1) * 128], rhs=W[:, kc * 512:(kc + 1) * 512],
                                start=(kg == 0 and kc == 0), stop=(kg == 3 and kc == 7)),
                                [wr, R("HT", kk)], [pr])
                        if kg == 3:
                            sc.op("dve", lambda e, i=i, hf=hf, pt=pt: e.tensor_tensor(
                                out=XA[:, i, hf * 512:(hf + 1) * 512], in0=pt[:, :], in1=XA[:, i, hf * 512:(hf + 1) * 512], op=ALU.add),
                                [pr, R("XA", i)], [R("XA", i)])
                    wprefetch()
            ln_group(NT, lambda i: store_rows(dst, p0 + i, i, R("xs", p0 + i) if dst is xs else R("y", p0 + i)))

        def a_head_load(l, t):
            p0 = t * NT
            e0 = 0 if t == 0 else 2
            head_load(xs, [min(max(p0 - 1 + e, 0), S - 1) for e in range(e0, 6)], "xs", k0=e0)

        def a_head_tr(l, t):
            if t > 0:
                sc.op("act", lambda e: e.activation(out=xT[:, :, 0:256], in_=xT[:, :, 512:768], func=AF.Copy), [R("xT")], [R("xT")])
                sc.op("act", lambda e: e.activation(out=KT[:, :, 0:256], in_=KT[:, :, 512:768], func=AF.Copy), [R("KT")], [R("KT")])
                sc.op("pool", lambda e: e.tensor_copy(out=VA[:, 0:2, :, 0:64], in_=VA[:, 4:6, :, 0:64]),
                      [R("VA", 4), R("VA", 5)], [R("VA", 0), R("VA", 1)])
                sc.op("pool", lambda e: e.tensor_copy(out=U[:, 0:2, :], in_=U[:, 4:6, :]),
                      [R("U", 4), R("U", 5)], [R("U", 0), R("U", 1)])
            head_tr(6, k0=0 if t == 0 else 2)

        def a_body1(l, t):
            p0 = t * NT
            if t == 0:
                load_vec(Gt, 5 * l + 0, "Gt")
                load_vec(Bt, 5 * l + 1, "Bt")
                sc.op("dve", lambda e: e.memset(QTz[:, :, :, :], 0.0), [], [R("QTz", jq) for jq in range(4)])
            e0 = 0 if t == 0 else 2
            W, wr = wnext(l, 0)
            for j in range(4):
                for (t0, n) in (((0, 512), (512, 256)) if t == 0 else ((256, 512),)):
                    bk = [0, 1, 6, 7, 2, 3, 4, 5][state["g"] % 8]
                    state["g"] += 1
                    pt, pr = psbank(bk)
                    for kc in range(8):
                        sc.op("pe", lambda e, kc=kc, j=j, t0=t0, n=n, W=W, pt=pt: e.matmul(
                            pt[:, 0:n], lhsT=W[:, kc * 512 + j * 128: kc * 512 + (j + 1) * 128],
                            rhs=xT[:, kc, t0:t0 + n], start=(kc == 0), stop=(kc == 7)),
                            [wr, R("xT")], [pr])
                    sc.op("act", lambda e, j=j, t0=t0, n=n, pt=pt: e.activation(out=KT[:, j, t0:t0 + n], in_=pt[:, 0:n], func=AF.Copy),
                          [pr], [R("KT")])
            wprefetch()
            W, wr = wnext(l, 1)
            for e_ in range(e0, 6):
                bk = [0, 1, 6, 7, 2, 3, 4, 5][state["g"] % 8]
                state["g"] += 1
                pt, pr = psbank(bk)
                for kc in range(8):
                    sc.op("pe", lambda e, kc=kc, e_=e_, W=W, pt=pt: e.matmul(
                        pt[:, :], lhsT=xT[:, kc, e_ * 128:(e_ + 1) * 128], rhs=W[:, kc * 512:(kc + 1) * 512],
                        start=(kc == 0), stop=(kc == 7)), [wr, R("xT")], [pr])
                sc.op("dve", lambda e, e_=e_, pt=pt: e.tensor_copy(out=U[:, e_, :], in_=pt[:, :]), [pr], [R("U", e_)])
            wprefetch()
            W, wr = wnext(l, 2)
            sc.op("act", lambda e, W=W: e.activation(out=WP[:], in_=W[:, 2048:2560], func=AF.Copy), [wr], [R("WP")])
            for e_ in range(e0, 6):
                bk = [0, 1, 6, 7, 2, 3, 4, 5][state["g"] % 8]
                state["g"] += 1
                pt, pr = psbank(bk)
                for kc in range(8):
                    sc.op("pe", lambda e, kc=kc, e_=e_, W=W, pt=pt: e.matmul(
                        pt[:, 0:256], lhsT=xT[:, kc, e_ * 128:(e_ + 1) * 128], rhs=W[:, kc * 256:(kc + 1) * 256],
                        start=(kc == 0), stop=(kc == 7)), [wr, R("xT")], [pr])
                sc.op("dve", lambda e, e_=e_, pt=pt: e.tensor_copy(
                    out=VA[:, e_, :, 0:64], in_=pt[:, 0:256].rearrange("p (j c) -> p j c", j=4)), [pr], [R("VA", e_)])
            if t == 0:
                bk = [0, 1, 6, 7, 2, 3, 4, 5][state["g"] % 8]
                state["g"] += 1
                pt, pr = psbank(bk)
                for kc in range(8):
                    sc.op("pe", lambda e, kc=kc, W=W, pt=pt: e.matmul(
                        pt[0:16, 0:256], lhsT=xT[:, kc, 128 + 112:256], rhs=W[:, kc * 256:(kc + 1) * 256],
                        start=(kc == 0), stop=(kc == 7)), [wr, R("xT")], [pr])
                sc.op("dve", lambda e, pt=pt: e.tensor_copy(
                    out=VAm[0:16, :, 0:64], in_=pt[0:16, 0:256].rearrange("p (j c) -> p j c", j=4)), [pr], [R("VAm")])
                sc.op("dve", lambda e: e.tensor_copy(out=KTm[:, :, :], in_=KT[:, :, 128 + 112:256]), [R("KT")], [R("KTm")])
            wprefetch()
            for qc in range(2):
                W, wr = wnext(l, 3 + qc)
                for mm in range(4):
                    m = qc * 4 + mm
                    bk = [0, 1, 6, 7, 2, 3, 4, 5][state["g"] % 8]
                    state["g"] += 1
                    pt, pr = psbank(bk)
                    for kc in range(8):
                        sc.op("pe", lambda e, kc=kc, mm=mm, W=W, pt=pt: e.matmul(
                            pt[:, :], lhsT=W[:, kc * 512 + mm * 128: kc * 512 + (mm + 1) * 128],
                            rhs=xT[:, kc, 128:640], start=(kc == 0), stop=(kc == 7)), [wr, R("xT")], [pr])
                    jq, hb = m // 2, 2 * (m % 2)
                    sc.op("act", lambda e, jq=jq, hb=hb, pt=pt: e.activation(
                        out=QTz[0:64, jq, :, hb * 128:(hb + 1) * 128], in_=pt[0:64, :].rearrange("p (i q) -> p i q", i=4),
                        func=AF.Copy), [pr], [R("QTz", jq)])
                    sc.op("act", lambda e, jq=jq, hb=hb, pt=pt: e.activation(
                        out=QTz[64:128, jq, :, (hb + 1) * 128:(hb + 2) * 128], in_=pt[64:128, :].rearrange("p (i q) -> p i q", i=4),
                        func=AF.Copy), [pr], [R("QTz", jq)])
                wprefetch()
            units = [(i, j) for i in range(NT) for j in range(4)]
            sctr = [0]

            def stage_a(u, i, j):
                pb = u % 3
                pos = p0 + i
                for c in range(4):
                    pa, pra = psbank(2 + sctr[0] % 4)
                    sctr[0] += 1
                    if c == 0:
                        nk = 16
                        lhs = KTm[:, j, :]
                        kres = R("KTm")
                    else:
                        nk = 128
                        e_ = i + c - 1
                        lhs = KT[:, j, e_ * 128:(e_ + 1) * 128]
                        kres = R("KT")
                    sc.op("pe", lambda e, pa=pa, lhs=lhs, nk=nk, c=c: e.matmul(
                        pa[0:nk, :], lhsT=lhs, rhs=QTz[:, j, i, :], start=True, stop=(c == 0)), [kres, R("QTz", j)], [pra])
                    if c == 0:
                        sc.op("act", lambda e, pa=pa: e.activation(
                            out=PT[pb][0:16, 0, :], in_=pa[0:16, :], func=AF.Exp, scale=0.125), [pra], [R("PT", pb, 0)])
                    else:
                        sc.op("pe", lambda e, pa=pa, c=c: e.matmul(
                            pa[:, :], lhsT=ident[:, :], rhs=alibi[:, c - 1, j, :], start=False, stop=True),
                            [R("ident"), R("alibi")], [pra])
                        sc.op("act", lambda e, pa=pa, c=c: e.activation(
                            out=PT[pb][:, c, :], in_=pa[:, :], func=AF.Exp, bias=pen[:, pos, c - 1:c], scale=0.125),
                            [pra, R("pen")], [R("PT", pb, c)])

            def stage_b(u, i, j):
                pb = u % 3
                rb = u % 2
                ib = i % 2
                po, pro = psbank(6 + u % 2)
                for blk in range(4):
                    for c in range(4):
                        if c == 0:
                            lhs = PT[pb][:, 0, blk * 128:(blk + 1) * 128]
                            rhs = VAm[:, j, 0:65]
                            rr = [R("VAm"), R("PT", pb), R("PT", pb, 0)]
                        else:
                            e_ = i + c - 1
                            lhs = PT[pb][:, c, blk * 128:(blk + 1) * 128]
                            rhs = VA[:, e_, j, 0:65]
                            rr = [R("VA", e_), R("VA"), R("PT", pb), R("PT", pb, c)]
                        sc.op("pe", lambda e, lhs=lhs, rhs=rhs, c=c, po=po, blk=blk: e.matmul(
                            po[:, blk * 65:(blk + 1) * 65], lhsT=lhs, rhs=rhs, start=(c == 0), stop=(c == 3)), rr, [pro])
                pov = po[:, 0:260].rearrange("p (b c) -> p b c", b=4)
                sc.op("dve", lambda e, pov=pov, rb=rb: e.tensor_tensor(
                    out=den[rb][:, :], in0=pov[:, :, 64], in1=esink[:, l * 16 + j * 4: l * 16 + j * 4 + 4], op=ALU.add),
                    [pro, R("esink")], [R("den", rb)])
                sc.op("dve", lambda e, rb=rb: e.reciprocal(out=rcp[rb][:, :], in_=den[rb][:, :]), [R("den", rb)], [R("rcp", rb)])
                for blk in range(4):
                    h = 4 * j + blk
                    sc.op("dve", lambda e, pov=pov, rb=rb, blk=blk, h=h, ib=ib: e.tensor_scalar(
                        out=ATtok[ib][:, h * 64:(h + 1) * 64], in0=pov[:, blk, 0:64], scalar1=rcp[rb][:, blk:blk + 1],
                        scalar2=None, op0=ALU.mult), [pro, R("rcp", rb)], [R("ATtok", ib)])
                if j == 3:
                    pending_tr.append([i, ib, 0])

            pending_tr = []

            def at_transposes(i, ib):
                if True:
                    pt, pr = psbank(state["g"] % 2)
                    state["g"] += 1
                    ptb = pt.bitcast(BF16)
                    for kc in range(8):
                        sc.op("pe", lambda e, kc=kc, ptb=ptb, ib=ib: e.transpose(
                            out=ptb[:, kc * 128:(kc + 1) * 128], in_=ATtok[ib][:, kc * 128:(kc + 1) * 128], identity=ident[:]),
                            [R("ATtok", ib), R("ident")], [pr])
                    sc.op("dve", lambda e, ptb=ptb: e.tensor_copy(
                        out=AT[:, :, i * 128:(i + 1) * 128], in_=ptb.rearrange("p (k t) -> p k t", k=8)),
                        [pr], [R("AT", k) for k in range(8)])

            def pool_blend(i):
                pos = p0 + i
                ab = i % 2
                if pos != 0:
                    sc.op("dve", lambda e, ab=ab, pos=pos: e.scalar_tensor_tensor(
                        out=Abl[ab][:, :, 120:128], in0=cp4(0, 32), scalar=nvp[:, pos:pos + 1], in1=cp4(32, 64),
                        op0=ALU.mult, op1=ALU.add), [R("cpool"), R("nvp")], [R("Abl", ab)])
                sc.op("dve", lambda e, ab=ab, pos=pos: e.tensor_scalar(
                    out=Anx[ab][:, :, :], in0=cp4(96, 128), scalar1=nvp[:, pos:pos + 1], scalar2=None, op0=ALU.mult),
                    [R("cpool"), R("nvp")], [R("Anx", ab)])

            def pool_slot(i):
                pos = p0 + i
                e_ = i + 1
                ab = i % 2
                if pos == 0:
                    aown = Ast
                    aown_r = R("Ast")
                else:
                    aown = Abl[ab]
                    aown_r = R("Abl", ab)
                pt, pr = psbank(state["g"] % 2)
                state["g"] += 1
                for g in range(4):
                    sc.op("pe", lambda e, g=g, aown=aown, pt=pt, e_=e_: e.matmul(
                        pt[:, g * 128:(g + 1) * 128], lhsT=U[:, e_, g * 128:(g + 1) * 128], rhs=aown[:, g, :],
                        start=True, stop=False), [R("U", e_), aown_r], [pr])
                    if pos != 0:
                        sc.op("pe", lambda e, g=g, pt=pt, e_=e_: e.matmul(
                            pt[:, g * 128:g * 128 + 8], lhsT=U[:, e_ - 1, g * 128:(g + 1) * 128], rhs=Apv[:, g, :],
                            start=False, stop=False), [R("U", e_ - 1), R("Apv")], [pr])
                    sc.op("pe", lambda e, g=g, pt=pt, e_=e_, ab=ab: e.matmul(
                        pt[:, g * 128 + 120:g * 128 + 128], lhsT=U[:, e_ + 1, g * 128:(g + 1) * 128], rhs=Anx[ab][:, g, :],
                        start=False, stop=True), [R("U", e_ + 1), R("Anx", ab)], [pr])
                sc.op("dve", lambda e, pt=pt, i=i: e.tensor_copy(
                    out=DT[:, :, i * 128:(i + 1) * 128], in_=pt[:, :].rearrange("p (g t) -> p g t", g=4)),
                    [pr], [R("DT")])
                if i + 1 < NT:
                    pool_blend(i + 1)

            wprefetch()
            if l == 0:
                convert(0, max(2, -(-NCH_M // ntiles)))
                convert(1, 1)
            else:
                convert(l + 1, 2)
            pool_blend(0)
            for u, (i, j) in enumerate(units):
                stage_a(u, i, j)
                for ent in pending_tr:
                    ent[2] += 1
                while pending_tr and pending_tr[0][2] >= 3:
                    ent = pending_tr.pop(0)
                    at_transposes(ent[0], ent[1])
                if u >= 2:
                    stage_b(u - 2, *units[u - 2])
                if u % 4 == 1:
                    pool_slot(u // 4)
            stage_b(len(units) - 2, *units[-2])
            stage_b(len(units) - 1, *units[-1])
            while pending_tr:
                ent = pending_tr.pop(0)
                at_transposes(ent[0], ent[1])
            for g in range(4):
                pt, pr = psbank(state["g"] % 2)
                state["g"] += 1
                sc.op("pe", lambda e, g=g, pt=pt: e.matmul(pt[:, :], lhsT=WP[:, g * 128:(g + 1) * 128], rhs=DT[:, g, :],
                                                           start=True, stop=True), [R("WP"), R("DT")], [pr])
                sc.op("act", lambda e, g=g, pt=pt: e.activation(out=YP[:, g, :], in_=pt[:, :], func=AF.Copy,
                                                                scale=pst[:, l * 4 + g: l * 4 + g + 1]),
                      [pr, R("pst")], [R("YP")])
            for m in range(8):
                W, wr = wnext(l, 5 + m)
                st = m % 2
                banks = [2, 3, 4, 5] if st == 0 else [6, 7, 0, 1]
                (pga, rga), (pya, rya), (pgb, rgb), (pyb, ryb) = [psbank(b) for b in banks]
                for kc in range(8):
                    sc.op("pe", lambda e, kc=kc, W=W, pga=pga: e.matmul(
                        pga[:, :], lhsT=W[:, kc * 128:(kc + 1) * 128], rhs=xT[:, kc, 128:640], start=(kc == 0), stop=(kc == 7)),
                        [wr, R("xT")], [rga])
                sc.op("act", lambda e, pga=pga, st=st: e.activation(out=TH[st][:, 0, :], in_=pga[:, :], func=AF.Tanh, scale=0.5),
                      [rga], [R("TH", st, 0)])
                for kc in range(8):
                    sc.op("pe", lambda e, kc=kc, W=W, pya=pya: e.matmul(
                        pya[:, :], lhsT=W[:, 2048 + kc * 128: 2048 + (kc + 1) * 128], rhs=AT[:, kc, :], start=(kc == 0), stop=(kc == 7)),
                        [wr, R("AT", kc)], [rya])
                sc.op("dve", lambda e, pya=pya, st=st: e.scalar_tensor_tensor(
                    out=TU[st][:], in0=TH[st][:, 0, :], scalar=1.0, in1=pya[:, :], op0=ALU.add, op1=ALU.mult),
                    [rya, R("TH", st, 0)], [R("TU", st)])
                for kc in range(8):
                    sc.op("pe", lambda e, kc=kc, W=W, pgb=pgb: e.matmul(
                        pgb[:, :], lhsT=W[:, 1024 + kc * 128: 1024 + (kc + 1) * 128], rhs=xT[:, kc, 128:640], start=(kc == 0), stop=(kc == 7)),
                        [wr, R("xT")], [rgb])
                sc.op("act", lambda e, pgb=pgb, st=st: e.activation(out=TH[st][:, 1, :], in_=pgb[:, :], func=AF.Tanh, scale=0.5),
                      [rgb], [R("TH", st, 1)])
                for kc in range(4):
                    sc.op("pe", lambda e, kc=kc, W=W, pyb=pyb: e.matmul(
                        pyb[:, :], lhsT=W[:, 3072 + kc * 128: 3072 + (kc + 1) * 128], rhs=YP[:, kc, :], start=(kc == 0), stop=(kc == 3)),
                        [wr, R("YP")], [ryb])
                sc.op("dve", lambda e, pyb=pyb, st=st: e.scalar_tensor_tensor(
                    out=TH[st][:, 1, :], in0=TH[st][:, 1, :], scalar=1.0, in1=pyb[:, :], op0=ALU.add, op1=ALU.mult),
                    [ryb, R("TH", st, 1)], [R("TH", st, 1)])
                sc.op("dve", lambda e, st=st, m=m: e.tensor_tensor(out=MT[:, m, :], in0=TU[st][:], in1=TH[st][:, 1, :], op=ALU.add),
                      [R("TU", st), R("TH", st, 1)], [R("MT", m)])
                wprefetch()

        def a_body2(l, t):
            p0 = t * NT
            for i in range(NT):
                pos = p0 + i
                sc.dma("pool", lambda e, pos=pos, i=i: e.dma_start(out=XA[:, i, :], in_=xs[pos * 128:(pos + 1) * 128, :]),
                       [R("xs", pos)], [R("XA", i)], f"xa{i}")
                sc.op("act", lambda e, i=i: e.activation(out=XA[:, i, :], in_=XA[:, i, :], func=AF.Copy, scale=ALPHA),
                      [R("XA", i)], [R("XA", i)])
            Ws = [wnext(l, 13), wnext(l, 14)]
            for i in range(NT):
                for hf in range(2):
                    W, wr = Ws[hf]
                    pt, pr = psbank([2, 3, 4, 5, 6, 7, 0, 1][state["g"] % 8])
                    state["g"] += 1
                    for kc in range(8):
                        sc.op("pe", lambda e, kc=kc, i=i, W=W, pt=pt: e.matmul(
                            pt[:, :], lhsT=MT[:, kc, i * 128:(i + 1) * 128], rhs=W[:, kc * 512:(kc + 1) * 512],
                            start=(kc == 0), stop=(kc == 7)), [wr, R("MT", kc)], [pr])
                    sc.op("dve", lambda e, i=i, hf=hf, pt=pt: e.scalar_tensor_tensor(
                        out=XA[:, i, hf * 512:(hf + 1) * 512], in0=pt[:, :], scalar=0.5,
                        in1=XA[:, i, hf * 512:(hf + 1) * 512], op0=ALU.mult, op1=ALU.add), [pr, R("XA", i)], [R("XA", i)])
            wprefetch()
            ln_group(NT, lambda i: store_rows(x1s, p0 + i, i, R("x1s", p0 + i)))

        load_consts()
        pass_emb()

        def head_ld(tk):
            (a_head_load if tk[0] == "A" else m_head_load)(tk[1], tk[2])

        def head_t(tk):
            (a_head_tr if tk[0] == "A" else m_head_tr)(tk[1], tk[2])

        def body1(tk):
            (a_body1 if tk[0] == "A" else m_body1)(tk[1], tk[2])

        def body2(tk):
            if tk[0] == "A":
                a_body2(tk[1], tk[2])
            else:
                m_body2(tk[1], tk[2], yout if tk[1] == L - 1 else xs)

        head_ld(tasks[0])
        head_t(tasks[0])
        for k, tk in enumerate(tasks):
            if k + 1 < len(tasks):
                head_ld(tasks[k + 1])
            if tk[0] == "M":
                done_in_a = min(NCH, (1 if tk[1] == 0 else 2) * ntiles)
                convert(tk[1] + 1, -(-(NCH - done_in_a) // ntiles))
            body1(tk)
            if k + 1 < len(tasks):
                head_t(tasks[k + 1])
            body2(tk)
        sc.final_wait("sp")

        with nc.Block() as block:
            sc.emit(nc, block, stack)
    return nc, sc


def _kcview(w):
    K, C = w.shape
    return w.reshape(K // 128, 128, C).transpose(1, 0, 2)


def build_wst(w_in, w_pool, w_bo_attn, w_bo_pool, w_out, w_mlp1, w_mlp2):
    L = w_in.shape[0]
    out = np.zeros((L, NCH, 128, CH), np.float32)
    for l in range(L):
        wi = _kcview(w_in[l])
        q, k, v, u = wi[:, :, 0:1024], wi[:, :, 1024:1280], wi[:, :, 1280:1536], wi[:, :, 1536:2048]
        ga, gb = wi[:, :, 2048:3072], wi[:, :, 3072:4096]
        kd = np.concatenate([k[:, :, j * 64:(j + 1) * 64] for j in range(4) for _ in range(2)], axis=2)
        out[l, 0] = kd.reshape(128, CH)
        out[l, 1] = u.reshape(128, CH)
        out[l, 2, :, 0:2048] = v.reshape(128, 2048)
        out[l, 2, :, 2048:2560] = w_pool[l].transpose(1, 0, 2).reshape(128, 512)
        out[l, 3] = q[:, :, 0:512].reshape(128, CH)
        out[l, 4] = q[:, :, 512:1024].reshape(128, CH)
        boa, bop = _kcview(w_bo_attn[l]), _kcview(w_bo_pool[l])
        for m in range(8):
            sl = slice(m * 128, (m + 1) * 128)
            out[l, 5 + m, :, 0:1024] = ga[:, :, sl].reshape(128, 1024)
            out[l, 5 + m, :, 1024:2048] = gb[:, :, sl].reshape(128, 1024)
            out[l, 5 + m, :, 2048:3072] = boa[:, :, sl].reshape(128, 1024)
            out[l, 5 + m, :, 3072:3584] = bop[:, :, sl].reshape(128, 512)
        wo = _kcview(w_out[l])
        out[l, 13] = wo[:, :, 0:512].reshape(128, CH)
        out[l, 14] = wo[:, :, 512:1024].reshape(128, CH)
        w1 = _kcview(w_mlp1[l])
        for c in range(8):
            out[l, NCH_A + c] = w1[:, :, c * 512:(c + 1) * 512].reshape(128, CH)
        w2 = _kcview(w_mlp2[l])
        for hf in range(2):
            for kg in range(4):
                out[l, NCH_A + 8 + hf * 4 + kg] = w2[:, kg * 8:(kg + 1) * 8, hf * 512:(hf + 1) * 512].reshape(128, CH)
    return out


def build_static_consts():
    slopes = (2.0 ** (-8.0 * np.arange(1, NH + 1) / NH)).astype(np.float32)
    jj = np.arange(128)[:, None].astype(np.float32)
    ii = np.arange(128)[None, :].astype(np.float32)
    alibi = np.zeros((128, 3, 4, 4, 128), np.float32)
    for j in range(4):
        for blk in range(4):
            h = 4 * j + blk
            s = slopes[h]
            alibi[:, 0, j, blk] = np.where(jj >= ii, -s * (128.0 + ii - jj), NEG)
            alibi[:, 1, j, blk] = -s * np.abs(jj - ii)
            alibi[:, 2, j, blk] = np.where(jj <= ii, -s * (128.0 + jj - ii), NEG)
    alibi = (8.0 * alibi).reshape(128, 3 * 4 * 512)

    def band(lo_clip=None, hi_clip=None):
        A = np.zeros((128, 4, 128), np.float64)
        for g, w in enumerate(POOL_W):
            for t in range(128):
                lo, hi = t - w // 2, t + w // 2
                lo_e = lo if lo_clip is None else max(lo, lo_clip)
                hi_e = hi if hi_clip is None else min(hi, hi_clip)
                if lo_clip is not None and t < lo_clip:
                    lo_e, hi_e = lo, hi
                cnt = hi_e - lo_e
                for s in range(max(lo_e, 0), min(hi_e, 128)):
                    A[s, g, t] += 1.0 / cnt
                A[t, g, t] -= 1.0
        return A

    A_norm = band()
    A_end = band(hi_clip=128)
    A_start = band(lo_clip=112)
    A_prev8 = np.zeros((128, 4, 8), np.float64)
    A_next8 = np.zeros((128, 4, 8), np.float64)
    for g, w in enumerate(POOL_W):
        for t in range(8):
            for sp_ in range(128):
                if sp_ - 128 >= t - w // 2:
                    A_prev8[sp_, g, t] = 1.0 / w
        for tt in range(8):
            t = 120 + tt
            for sp_ in range(128):
                if 128 + sp_ <= t + w // 2 - 1:
                    A_next8[sp_, g, tt] = 1.0 / w
    cpool = np.zeros((128, 1152), np.float32)
    cpool[:, 0:512] = A_norm.reshape(128, 512)
    cpool[:, 512:1024] = A_start.reshape(128, 512)
    cpool[:, 1024:1056] = (A_norm[:, :, 120:128] - A_end[:, :, 120:128]).reshape(128, 32)
    cpool[:, 1056:1088] = A_end[:, :, 120:128].reshape(128, 32)
    cpool[:, 1088:1120] = A_prev8.reshape(128, 32)
    cpool[:, 1120:1152] = A_next8.reshape(128, 32)
    return alibi, cpool


def chain_tables(blocks, ndummy, S):
    assert 1 + len(blocks) + ndummy == S
    pv = np.zeros(S, np.float32)
    ov = np.ones(S, np.float32)
    nv = np.zeros(S, np.float32)
    ov[0] = 0.0
    nv[0] = 1.0 if (len(blocks) and blocks[0] == 0) else 0.0
    for p in range(1, 1 + len(blocks)):
        b = blocks[p - 1]
        if p >= 2 and blocks[p - 2] == b - 1:
            pv[p] = 1.0
        if p < len(blocks) and blocks[p] == b + 1:
            nv[p] = 1.0
    pen = np.stack([pv, ov, nv], axis=1)
    pen = (pen - 1.0) * (-NEG)
    pen_t = np.broadcast_to(pen.reshape(1, S * 3), (128, S * 3)).astype(np.float32).copy()
    nvp_t = np.broadcast_to(nv.reshape(1, S), (128, S)).astype(np.float32).copy()
    return pen_t, nvp_t


def build_xin(x_seq, meta_tokens, blocks, ndummy, S):
    xin = np.zeros((S * 128, D), np.float32)
    xin[112:128] = meta_tokens
    for p, b in enumerate(blocks):
        xin[(p + 1) * 128:(p + 2) * 128] = x_seq[b * 128:(b + 1) * 128]
    return xin


def build_vec_consts(L, ln_emb_g, ln_emb_b, sink, pool_scale, ln1_g, ln1_b, b_mlp1, b_mlp2, ln2_g, ln2_b):
    c_vec = np.zeros((5 * L + 2, D), np.float32)
    for l in range(L):
        c_vec[5 * l + 0] = ln1_g[l]
        c_vec[5 * l + 1] = ln1_b[l]
        c_vec[5 * l + 2] = ln2_g[l]
        c_vec[5 * l + 3] = ln2_b[l]
        c_vec[5 * l + 4] = b_mlp2[l]
    c_vec[5 * L] = ln_emb_g
    c_vec[5 * L + 1] = ln_emb_b
    c_b1 = np.ascontiguousarray(b_mlp1[:L].reshape(L, 32, 128).transpose(2, 0, 1).reshape(128, L * 32))
    c_ps = np.ascontiguousarray(pool_scale[:L].reshape(L, 4, 128).transpose(2, 0, 1).reshape(128, L * 4))
    perm = np.arange(16)
    c_sink = np.ascontiguousarray(sink[:L][:, perm].reshape(1, L * 16))
    return c_vec, c_b1, c_ps, c_sink


S_FULL = 40
_PROG_CACHE = {}


def core_plan():
    plan = []
    for b in range(2):
        plan.append(("p", b, list(range(0, 36)), 3, list(range(0, 32))))
        plan.append(("p", b, [0, 1, 2] + list(range(28, 64)), 0, list(range(32, 64))))
    for b in range(4):
        plan.append(("s", b, list(range(32)), 7, list(range(32))))
    return plan


def kernel(x_prompt, x_sample, meta_tokens, ln_emb_g, ln_emb_b, w_in, sink, w_pool, pool_scale,
           w_bo_attn, w_bo_pool, w_out, ln1_g, ln1_b, w_mlp1, b_mlp1, w_mlp2, b_mlp2, ln2_g, ln2_b):
    f = lambda a: np.asarray(a, dtype=np.float32)
    x_prompt, x_sample, meta_tokens = f(x_prompt), f(x_sample), f(meta_tokens)
    L = DEPTH
    S = S_FULL
    if "nc" not in _PROG_CACHE:
        _PROG_CACHE["nc"] = build_program(S, L)[0]
    nc = _PROG_CACHE["nc"]
    wst = build_wst(f(w_in), f(w_pool), f(w_bo_attn), f(w_bo_pool), f(w_out), f(w_mlp1), f(w_mlp2))
    alibi, cpool = build_static_consts()
    c_vec, c_b1, c_ps, c_sink = build_vec_consts(L, f(ln_emb_g), f(ln_emb_b), f(sink), f(pool_scale), f(ln1_g), f(ln1_b),
                                                 f(b_mlp1), f(b_mlp2), f(ln2_g), f(ln2_b))
    plan = core_plan()
    in_maps = []
    for (src, b, blocks, ndummy, own) in plan:
        xseq = x_prompt[b] if src == "p" else x_sample[b]
        pen_t, nvp_t = chain_tables(blocks, ndummy, S)
        in_maps.append({
            "xin": build_xin(xseq, meta_tokens, blocks, ndummy, S), "wst": wst, "c_alibi": alibi, "c_pen": pen_t,
            "c_nvp": nvp_t, "c_pool": cpool, "c_vec": c_vec, "c_b1": c_b1, "c_ps": c_ps, "c_sink": c_sink,
        })
    res = run_bass_kernel_spmd(nc, in_maps, core_ids=list(range(8)))
    y_prompt = np.zeros(x_prompt.shape, np.float32)
    y_sample = np.zeros(x_sample.shape, np.float32)
    for ci, (src, b, blocks, ndummy, own) in enumerate(plan):
        y = res.results[ci]["y"]
        dst = y_prompt if src == "p" else y_sample
        for p, blk in enumerate(blocks):
            if blk in own:
                dst[b, blk * 128:(blk + 1) * 128] = y[(p + 1) * 128:(p + 2) * 128]
    return (y_prompt, y_sample)
```

```python
import contextlib
import numpy as np
import concourse.bass as bass
import concourse.mybir as mybir
from concourse.bass_utils import run_bass_kernel_spmd

F32 = mybir.dt.float32
BF16 = mybir.dt.bfloat16
ALU = mybir.AluOpType
AF = mybir.ActivationFunctionType

D = 1024
DEPTH = 4
NH = 16
NKV = 4
N_META = 16
ALPHA = float((2 * DEPTH) ** 0.25)
LN_EPS = 1e-5
NEG = -30000.0
POOL_W = (2, 4, 8, 16)
NCH_A = 15
NCH_M = 16
NCH = NCH_A + NCH_M
CH = 4096
ENGS = ("pe", "act", "dve", "pool", "sp")
EPOCH = 16000
import os
USE_FP32_STREAM = bool(os.environ.get("FP32STREAM"))


class Sched:
    def __init__(self):
        self.ops = {e: [] for e in ENGS}
        self.cnt = {e: 0 for e in ENGS}
        self.known = {e: {} for e in ENGS}
        self.lastw = {}
        self.readers = {}
        self.dmacnt = {}
        self.snap = {}
        self.nwaits = 0

    def _deps(self, eng, reads, writes, is_dma):
        deps = {}

        def add(p, kind):
            if (not is_dma) and p[0] == "e" and p[1] == eng and kind != "raw":
                return
            k = (p[0], p[1])
            if deps.get(k, 0) < p[2]:
                deps[k] = p[2]

        for r in reads:
            w = self.lastw.get(r)
            if w is not None:
                add(w, "raw")
        for w_ in writes:
            w = self.lastw.get(w_)
            if w is not None:
                add(w, "waw")
            for rd in self.readers.get(w_, ()):
                add(rd, "war")
        return deps

    def _filter(self, eng, deps):
        kn = self.known[eng]
        waits = []
        for k, v in deps.items():
            if kn.get(k, 0) >= v:
                continue
            waits.append((k, v))
        for k, v in waits:
            if kn.get(k, 0) < v:
                kn[k] = v
            sn = self.snap.get((k[0], k[1], v))
            if sn:
                for k2, v2 in sn.items():
                    if kn.get(k2, 0) < v2:
                        kn[k2] = v2
        self.nwaits += len(waits)
        return waits

    def _update(self, me, reads, writes):
        for r in reads:
            self.readers.setdefault(r, []).append(me)
        for w_ in writes:
            self.lastw[w_] = me
            self.readers[w_] = []

    def op(self, eng, fn, reads=(), writes=()):
        deps = self._deps(eng, reads, writes, False)
        waits = self._filter(eng, deps)
        self.cnt[eng] += 1
        seq = self.cnt[eng]
        self.ops[eng].append(("op", waits, fn, seq))
        self.snap[("e", eng, seq)] = dict(self.known[eng])
        self._update(("e", eng, seq), reads, writes)

    def dma(self, q, fn, reads, writes, key, serialize=False):
        deps = self._deps(q, reads, writes, True)
        if serialize and self.dmacnt.get(key, 0) > 0:
            deps[("d", key)] = max(deps.get(("d", key), 0), self.dmacnt[key])
        waits = self._filter(q, deps)
        v = self.dmacnt.get(key, 0) + 16
        self.dmacnt[key] = v
        self.ops[q].append(("dma", waits, fn, key))
        self.snap[("d", key, v)] = dict(self.known[q])
        self._update(("d", key, v), reads, writes)

    def final_wait(self, eng):
        waits = []
        for key, v in self.dmacnt.items():
            if self.known[eng].get(("d", key), 0) < v:
                waits.append((("d", key), v))
        self.ops[eng].append(("wait", waits, None, None))

    def emit(self, nc, block, stack):
        esem = {}
        for e in ENGS:
            n = (self.cnt[e] + EPOCH - 1) // EPOCH
            esem[e] = [stack.enter_context(nc.semaphore(f"s_{e}_{i}")) for i in range(max(n, 1))]
        dsem = {}
        for i, key in enumerate(self.dmacnt):
            dsem[key] = stack.enter_context(nc.semaphore(f"d_{i}"))

        def do_waits(eng, waits):
            for (k, v) in waits:
                if k[0] == "e":
                    eng.wait_ge(esem[k[1]][(v - 1) // EPOCH], (v - 1) % EPOCH + 1)
                else:
                    eng.wait_ge(dsem[k[1]], v)

        def run(ename):
            def body(eng):
                for rec in self.ops[ename]:
                    kind, waits, fn, extra = rec
                    do_waits(eng, waits)
                    if kind == "op":
                        inst = fn(eng)
                        inst.then_inc(esem[ename][(extra - 1) // EPOCH], 1)
                    elif kind == "dma":
                        inst = fn(eng)
                        inst.then_inc(dsem[extra], 16)
            return body

        block.tensor(run("pe"))
        block.scalar(run("act"))
        block.vector(run("dve"))
        block.gpsimd(run("pool"))
        block.sync(run("sp"))


def build_program(S, L, debug_layers=None):
    NT = 4
    assert S % NT == 0
    ntiles = S // NT
    nc = bass.Bass("TRN2", target_bir_lowering=False)
    sc = Sched()

    def din(name, shape, dt=F32):
        return nc.dram_tensor(name, shape, dt, kind="ExternalInput").ap()

    xin = din("xin", [S * 128, D])
    wst = din("wst", [L, NCH, 128, CH])
    c_alibi = din("c_alibi", [128, 3 * 4 * 512])
    c_pen = din("c_pen", [128, S * 3])
    c_nvp = din("c_nvp", [128, S])
    c_pool = din("c_pool", [128, 1152])
    c_vec = din("c_vec", [5 * L + 2, D])
    c_b1 = din("c_b1", [128, L * 32])
    c_ps = din("c_ps", [128, L * 4])
    c_sink = din("c_sink", [1, L * 16])
    yout = nc.dram_tensor("y", [S * 128, D], F32, kind="ExternalOutput").ap()
    xs = nc.dram_tensor("xs", [S * 128, D], F32, kind="Internal").ap()
    x1s = nc.dram_tensor("x1s", [S * 128, D], F32, kind="Internal").ap()
    wbf = nc.dram_tensor("wbf", [L, NCH, 128, CH], BF16, kind="Internal").ap()

    stack = contextlib.ExitStack()
    with stack:
        def sb(name, shape, dt):
            return stack.enter_context(nc.sbuf_tensor(name, shape, dt))

        NRING = 5
        ring = [sb(f"ring{i}", [128, CH], BF16) for i in range(NRING)]
        xb = [sb(f"xb{i}", [128, D], BF16) for i in range(6)]
        XA = sb("XA", [128, 4, D], F32)
        xT = sb("xT", [128, 8, 768], BF16)
        arB = sb("arB", [128, 18432], BF16)
        KT = arB[:, 0:3072].rearrange("p (j t) -> p j t", j=4)
        QTz = arB[:, 3072:11264].rearrange("p (j i c) -> p j i c", j=4, i=4)
        U = arB[:, 11264:14336].rearrange("p (e c) -> p e c", e=6)
        MT = arB[:, 14336:18432].rearrange("p (m t) -> p m t", m=8)
        HT = arB[:, 0:16384].rearrange("p (m t) -> p m t", m=32)
        VA = sb("VA", [128, 6, 4, 128], BF16)
        PT = [sb(f"PT{i}", [128, 4, 512], BF16) for i in range(3)]
        AT = sb("AT", [128, 8, 512], BF16)
        DT = sb("DT", [128, 4, 512], BF16)
        YP = sb("YP", [128, 4, 512], BF16)
        WP = sb("WP", [128, 512], BF16)
        KTm = sb("KTm", [128, 4, 16], BF16)
        VAm = sb("VAm", [128, 4, 128], BF16)
        Abl = [sb(f"Abl{i}", [128, 4, 128], BF16) for i in range(2)]
        Anx = [sb(f"Anx{i}", [128, 4, 8], BF16) for i in range(2)]
        Ast = sb("Ast", [128, 4, 128], BF16)
        Apv = sb("Apv", [128, 4, 8], BF16)
        ident = sb("ident", [128, 128], BF16)
        NTF = 3
        tmpF = [sb(f"tmpF{i}", [128, 512], F32) for i in range(NTF)]
        den = [sb(f"den{i}", [128, 4], F32) for i in range(2)]
        rcp = [sb(f"rcp{i}", [128, 4], F32) for i in range(2)]
        ATtok = [sb(f"ATtok{i}", [128, D], BF16) for i in range(2)]
        esink = sb("esink", [128, L * 16], F32)
        TH = [sb(f"TH{i}", [128, 2, 512], F32) for i in range(2)]
        TU = [sb(f"TU{i}", [128, 512], F32) for i in range(2)]
        B2t = TH[0][:, :, :].rearrange("p a b -> p (a b)")
        alibi = sb("alibi", [128, 3, 4, 512], BF16)
        pen = sb("pen", [128, S, 3], F32)
        nvp = sb("nvp", [128, S], F32)
        cpool = sb("cpool", [128, 128], F32)
        Gt = sb("Gt", [128, D], F32)
        Bt = sb("Bt", [128, D], F32)
        b1t = sb("b1t", [128, L * 32], F32)
        pst = sb("pst", [128, L * 4], F32)
        stats = sb("stats", [128, 8, 2, 6], F32)
        mv = sb("mv", [128, 8, 2], F32)
        sd = sb("sd", [128, 8], F32)
        rstd = sb("rstd", [128, 8], F32)
        nmr = sb("nmr", [128, 8], F32)
        psall = stack.enter_context(nc.psum_tensor("psall", [128, 4096], F32))
        ps = [psall[:, k * 512:(k + 1) * 512] for k in range(8)]

        state = {"xst": 0, "xb": 0, "g": 0, "tf": 0}

        def R(*a):
            return tuple(a)

        def cp4(a, b):
            return cpool[:, a:b].rearrange("p (g t) -> p g t", g=4)

        def load_consts():
            sc.dma("pool", lambda e: e.dma_start(out=alibi[:].rearrange("p a b c -> p (a b c)"), in_=c_alibi[:, :]),
                   [], [R("alibi")], "c0")
            sc.dma("sp", lambda e: e.dma_start(out=pen[:].rearrange("p a b -> p (a b)"), in_=c_pen[:, :]),
                   [], [R("pen")], "c1")
            sc.dma("sp", lambda e: e.dma_start(out=nvp[:], in_=c_nvp[:, :]), [], [R("nvp")], "c2")
            sc.dma("sp", lambda e: e.dma_start(out=cpool[:], in_=c_pool[:, 1024:1152]), [], [R("cpool")], "c3")
            sc.dma("sp", lambda e: e.dma_start(out=b1t[:], in_=c_b1[:, :]), [], [R("b1t")], "c4")
            sc.dma("sp", lambda e: e.dma_start(out=pst[:], in_=c_ps[:, :]), [], [R("pst")], "c5")
            sc.dma("sp", lambda e: e.dma_start(out=esink[:], in_=c_sink[0:1, :].partition_broadcast(128)), [], [R("esink")], "c6")
            sc.op("act", lambda e: e.activation(out=esink[:], in_=esink[:], func=AF.Exp), [R("esink")], [R("esink")])
            sc.op("pool", lambda e: e.memset(ident[:], 0.0), [], [R("ident")])
            sc.op("pool", lambda e: e.affine_select(out=ident[:], in_=ident[:], pattern=[[-1, 128]],
                                                    compare_op=ALU.not_equal, fill=1.0, base=0,
                                                    channel_multiplier=1), [R("ident")], [R("ident")])
            for b in range(3):
                sc.op("pool", lambda e, b=b: e.memset(PT[b][:, 0, :], 0.0), [], [R("PT", b)])
            sc.op("pool", lambda e: e.memset(VAm[:], 0.0), [], [R("VAm")])
            sc.op("pool", lambda e: e.memset(VAm[0:16, :, 64:128], 1.0), [R("VAm")], [R("VAm")])
            sc.op("pool", lambda e: e.memset(VA[:, :, :, 64:128], 1.0), [], [R("VA")])
            for b in range(2):
                sc.dma("pool", lambda e, b=b: e.dma_start(out=Abl[b][:].rearrange("p g t -> p (g t)"), in_=c_pool[:, 0:512]),
                       [], [R("Abl", b)], f"cA{b}")
            sc.dma("pool", lambda e: e.dma_start(out=Ast[:].rearrange("p g t -> p (g t)"), in_=c_pool[:, 512:1024]),
                   [], [R("Ast")], "cA2")
            sc.op("dve", lambda e: e.tensor_copy(out=Apv[:], in_=cp4(64, 96)), [R("cpool")], [R("Apv")])

        def load_vec(tile, row, tag):
            sc.dma("pool", lambda e: e.dma_start(out=tile[:], in_=c_vec[row:row + 1, :].partition_broadcast(128)),
                   [], [R(tag)], "v_" + tag)

        tasks = []
        for l in range(L):
            tasks += [("A", l, t) for t in range(ntiles)]
            tasks += [("M", l, t) for t in range(ntiles)]
        wsched = []
        for (k, l, t) in tasks:
            wsched += [(l, c) for c in (range(0, NCH_A) if k == "A" else range(NCH_A, NCH))]
        wst_state = {"issued": 0, "taken": 0}

        def _wissue():
            k = wst_state["issued"]
            if k >= len(wsched):
                return
            l, c = wsched[k]
            slot = k % NRING
            t = ring[slot]
            if l == 0 and c < NCH_A:
                sc.dma("pool", lambda e: e.dma_start(out=t[:], in_=wst[l, c, :, :]), [], [R("ring", slot)], f"wq{slot}")
            else:
                sc.dma("sp", lambda e: e.dma_start(out=t[:], in_=wbf[l, c, :, :]), [R("wbf", l, c)], [R("ring", slot)], f"w{slot}")
            wst_state["issued"] += 1

        cv_state = {"n": 0, "next": {0: NCH_A}}

        def convert(l, n):
            if l >= L:
                return
            c0 = cv_state["next"].get(l, 0)
            for c in range(c0, min(c0 + n, NCH)):
                key = f"cv{cv_state['n'] % 8}"
                cv_state["n"] += 1
                sc.dma("pool", lambda e, c=c: e.dma_start(
                    out=wbf[l, c, :, :].rearrange("(a b) f -> a (b f)", a=16),
                    in_=wst[l, c, :, :].rearrange("(a b) f -> a (b f)", a=16)), [], [R("wbf", l, c)], key, serialize=True)
            cv_state["next"][l] = min(c0 + n, NCH)

        def wnext(l, c):
            k = wst_state["taken"]
            assert wsched[k] == (l, c), (wsched[k], l, c)
            while wst_state["issued"] <= k:
                _wissue()
            wst_state["taken"] += 1
            return ring[k % NRING], R("ring", k % NRING)

        def wprefetch():
            while wst_state["issued"] < wst_state["taken"] + NRING:
                if wst_state["issued"] >= len(wsched):
                    break
                _wissue()

        def psbank(k):
            return ps[k], R("ps", k)

        def head_load(src, positions, res_name, k0=0):
            for k, pos in zip(range(k0, k0 + len(positions)), positions):
                sc.dma("pool", lambda e, k=k, pos=pos: e.dma_start(out=xb[k][:], in_=src[pos * 128:(pos + 1) * 128, :]),
                       [R(res_name, pos)], [R("xb", k)], f"xb{k}")

        def head_tr(n, k0=0):
            for k in range(k0, n):
                pt, pr = psbank(2 + state["g"] % 4)
                state["g"] += 1
                ptb = pt.bitcast(BF16)
                for kc in range(8):
                    sc.op("pe", lambda e, kc=kc, k=k, ptb=ptb: e.transpose(out=ptb[:, kc * 128:(kc + 1) * 128],
                                                                          in_=xb[k][:, kc * 128:(kc + 1) * 128], identity=ident[:]),
                          [R("xb", k), R("ident")], [pr])
                sc.op("dve", lambda e, k=k, ptb=ptb: e.tensor_copy(out=xT[:, :, k * 128:(k + 1) * 128],
                                                                   in_=ptb.rearrange("p (k t) -> p k t", k=8)),
                      [pr], [R("xT")])

        def ln_slot(i, after_slot):
            for h in range(2):
                sc.op("dve", lambda e, h=h: e.bn_stats(out=stats[:, i, h, :], in_=XA[:, i, h * 512:(h + 1) * 512]),
                      [R("XA", i)], [R("stats", i)])
            sc.op("dve", lambda e: e.bn_aggr(out=mv[:, i, :], in_=stats[:, i, :, :].rearrange("p a b -> p (a b)")),
                  [R("stats", i)], [R("mv", i)])
            sc.op("act", lambda e: e.activation(out=sd[:, i:i + 1], in_=mv[:, i, 1:2], func=AF.Sqrt, bias=LN_EPS, scale=1.0),
                  [R("mv", i)], [R("sd", i)])
            sc.op("dve", lambda e: e.reciprocal(out=rstd[:, i:i + 1], in_=sd[:, i:i + 1]), [R("sd", i)], [R("rstd", i)])
            sc.op("dve", lambda e: e.scalar_tensor_tensor(out=nmr[:, i:i + 1], in0=mv[:, i, 0:1], scalar=-1.0,
                                                          in1=rstd[:, i:i + 1], op0=ALU.mult, op1=ALU.mult),
                  [R("mv", i), R("rstd", i)], [R("nmr", i)])
            sc.op("act", lambda e: e.activation(out=XA[:, i, :], in_=XA[:, i, :], func=AF.Identity,
                                                scale=rstd[:, i:i + 1], bias=nmr[:, i:i + 1]),
                  [R("XA", i), R("rstd", i), R("nmr", i)], [R("XA", i)])
            sc.op("dve", lambda e: e.tensor_tensor(out=XA[:, i, :], in0=XA[:, i, :], in1=Gt[:], op=ALU.mult),
                  [R("XA", i), R("Gt")], [R("XA", i)])
            sc.op("pool", lambda e: e.tensor_tensor(out=XA[:, i, :], in0=XA[:, i, :], in1=Bt[:], op=ALU.add),
                  [R("XA", i), R("Bt")], [R("XA", i)])
            after_slot(i)

        def ln_group(n, after_slot):
            for i in range(n):
                for h in range(2):
                    sc.op("dve", lambda e, i=i, h=h: e.bn_stats(out=stats[:, i, h, :], in_=XA[:, i, h * 512:(h + 1) * 512]),
                          [R("XA", i)], [R("stats", i)])
                sc.op("dve", lambda e, i=i: e.bn_aggr(out=mv[:, i, :], in_=stats[:, i, :, :].rearrange("p a b -> p (a b)")),
                      [R("stats", i)], [R("mv", i)])
            for i in range(n):
                sc.op("act", lambda e, i=i: e.activation(out=sd[:, i:i + 1], in_=mv[:, i, 1:2], func=AF.Sqrt, bias=LN_EPS, scale=1.0),
                      [R("mv", i)], [R("sd", i)])
            for i in range(n):
                sc.op("dve", lambda e, i=i: e.reciprocal(out=rstd[:, i:i + 1], in_=sd[:, i:i + 1]), [R("sd", i)], [R("rstd", i)])
                sc.op("dve", lambda e, i=i: e.scalar_tensor_tensor(out=nmr[:, i:i + 1], in0=mv[:, i, 0:1], scalar=-1.0,
                                                                   in1=rstd[:, i:i + 1], op0=ALU.mult, op1=ALU.mult),
                      [R("mv", i), R("rstd", i)], [R("nmr", i)])
            for i in range(n):
                sc.op("act", lambda e, i=i: e.activation(out=XA[:, i, :], in_=XA[:, i, :], func=AF.Identity,
                                                         scale=rstd[:, i:i + 1], bias=nmr[:, i:i + 1]),
                      [R("XA", i), R("rstd", i), R("nmr", i)], [R("XA", i)])
            for i in range(n):
                sc.op("dve", lambda e, i=i: e.tensor_tensor(out=XA[:, i, :], in0=XA[:, i, :], in1=Gt[:], op=ALU.mult),
                      [R("XA", i), R("Gt")], [R("XA", i)])
            for i in range(n):
                sc.op("pool", lambda e, i=i: e.tensor_tensor(out=XA[:, i, :], in0=XA[:, i, :], in1=Bt[:], op=ALU.add),
                      [R("XA", i), R("Bt")], [R("XA", i)])
                after_slot(i)

        def store_rows(dst, pos, i, dst_res):
            sc.dma("pool", lambda e: e.dma_start(out=dst[pos * 128:(pos + 1) * 128, :], in_=XA[:, i, :]),
                   [R("XA", i)], [dst_res], f"st{i}")

        def pass_emb():
            load_vec(Gt, 5 * L, "Gt")
            load_vec(Bt, 5 * L + 1, "Bt")
            arF = arB[:, :].bitcast(F32)
            bufs = [arF[:, k * D:(k + 1) * D] for k in range(8)]
            for t in range(ntiles):
                ks = [(t % 2) * 4 + i for i in range(NT)]
                poss = [t * NT + i for i in range(NT)]
                for k, pos in zip(ks, poss):
                    sc.dma("sp", lambda e, k=k, pos=pos: e.dma_start(out=bufs[k], in_=xin[pos * 128:(pos + 1) * 128, :]),
                           [], [R("p0", k)], f"p0l{k}")
                for k in ks:
                    for h in range(2):
                        sc.op("dve", lambda e, k=k, h=h: e.bn_stats(out=stats[:, k, h, :], in_=bufs[k][:, h * 512:(h + 1) * 512]),
                              [R("p0", k)], [R("stats", k)])
                    sc.op("dve", lambda e, k=k: e.bn_aggr(out=mv[:, k, :], in_=stats[:, k, :, :].rearrange("p a b -> p (a b)")),
                          [R("stats", k)], [R("mv", k)])
                for k in ks:
                    sc.op("act", lambda e, k=k: e.activation(out=sd[:, k:k + 1], in_=mv[:, k, 1:2], func=AF.Sqrt, bias=LN_EPS, scale=1.0),
                          [R("mv", k)], [R("sd", k)])
                for k in ks:
                    sc.op("dve", lambda e, k=k: e.reciprocal(out=rstd[:, k:k + 1], in_=sd[:, k:k + 1]), [R("sd", k)], [R("rstd", k)])
                    sc.op("dve", lambda e, k=k: e.scalar_tensor_tensor(out=nmr[:, k:k + 1], in0=mv[:, k, 0:1], scalar=-1.0,
                                                                       in1=rstd[:, k:k + 1], op0=ALU.mult, op1=ALU.mult),
                          [R("mv", k), R("rstd", k)], [R("nmr", k)])
                for k in ks:
                    sc.op("act", lambda e, k=k: e.activation(out=bufs[k], in_=bufs[k], func=AF.Identity,
                                                             scale=rstd[:, k:k + 1], bias=nmr[:, k:k + 1]),
                          [R("p0", k), R("rstd", k), R("nmr", k)], [R("p0", k)])
                for k in ks:
                    sc.op("dve", lambda e, k=k: e.tensor_tensor(out=bufs[k], in0=bufs[k], in1=Gt[:], op=ALU.mult),
                          [R("p0", k), R("Gt")], [R("p0", k)])
                for k, pos in zip(ks, poss):
                    sc.op("pool", lambda e, k=k: e.tensor_tensor(out=bufs[k], in0=bufs[k], in1=Bt[:], op=ALU.add),
                          [R("p0", k), R("Bt")], [R("p0", k)])
                    sc.dma("pool", lambda e, k=k, pos=pos: e.dma_start(out=xs[pos * 128:(pos + 1) * 128, :], in_=bufs[k]),
                           [R("p0", k)], [R("xs", pos)], f"p0s{k}")
            allp0 = [R("p0", k) for k in range(8)]
            sc.op("dve", lambda e: e.memset(stats[:, 0, 0, 0:1], 0.0), [], allp0 + [R("stats", 0)])
            sc.op("act", lambda e: e.activation(out=sd[:, 0:1], in_=sd[:, 0:1], func=AF.Copy), [], allp0 + [R("sd", 0)])

        def m_head_load(l, t):
            head_load(x1s, [t * NT + i for i in range(NT)], "x1s")

        def m_head_tr(l, t):
            head_tr(NT)

        def m_body1(l, t):
            p0 = t * NT
            if t == 0:
                load_vec(Gt, 5 * l + 2, "Gt")
                load_vec(Bt, 5 * l + 3, "Bt")
                sc.dma("pool", lambda e: e.dma_start(out=B2t, in_=c_vec[5 * l + 4:5 * l + 5, :].partition_broadcast(128)),
                       [], [R("B2t"), R("TH", 0, 0), R("TH", 0, 1)], "v_B2t")
            for c in range(8):
                W, wr = wnext(l, NCH_A + c)
                for mm in range(4):
                    m = 4 * c + mm
                    bk = [0, 1, 6, 7, 2, 3, 4, 5][state["g"] % 8]
                    state["g"] += 1
                    pt, pr = psbank(bk)
                    for kc in range(8):
                        sc.op("pe", lambda e, kc=kc, mm=mm, W=W, pt=pt: e.matmul(
                            pt[:, :], lhsT=W[:, kc * 512 + mm * 128: kc * 512 + (mm + 1) * 128],
                            rhs=xT[:, kc, 0:512], start=(kc == 0), stop=(kc == 7)),
                            [wr, R("xT")], [pr])
                    tb = state["tf"] % NTF
                    state["tf"] += 1
                    sc.op("act", lambda e, pt=pt, tb=tb, m=m: e.activation(
                        out=tmpF[tb][:], in_=pt[:, :], func=AF.Relu, bias=b1t[:, l * 32 + m: l * 32 + m + 1], scale=1.0),
                        [pr, R("b1t")], [R("tmpF", tb)])
                    sc.op("dve", lambda e, tb=tb, m=m: e.tensor_tensor(out=HT[:, m, :], in0=tmpF[tb][:], in1=tmpF[tb][:], op=ALU.mult),
                          [R("tmpF", tb)], [R("HT", m)])
                wprefetch()

        def m_body2(l, t, dst):
            p0 = t * NT
            for i in range(NT):
                pos = p0 + i
                sc.dma("pool", lambda e, pos=pos, i=i: e.dma_start(out=XA[:, i, :], in_=x1s[pos * 128:(pos + 1) * 128, :]),
                       [R("x1s", pos)], [R("XA", i)], f"xa{i}")
                sc.op("act", lambda e, i=i: e.activation(out=XA[:, i, :], in_=XA[:, i, :], func=AF.Copy, scale=ALPHA),
                      [R("XA", i)], [R("XA", i)])
                sc.op("pool", lambda e, i=i: e.tensor_tensor(out=XA[:, i, :], in0=XA[:, i, :], in1=B2t, op=ALU.add),
                      [R("XA", i), R("B2t"), R("TH", 0, 0), R("TH", 0, 1)], [R("XA", i)])
            for hf in range(2):
                for kg in range(4):
                    W, wr = wnext(l, NCH_A + 8 + hf * 4 + kg)
                    for i in range(NT):
                        pt, pr = psbank(2 + i)
                        for kc in range(8):
                            kk = kg * 8 + kc
                            sc.op("pe", lambda e, kc=kc, kk=kk, i=i, W=W, pt=pt, kg=kg: e.matmul(
                                pt[:, :], lhsT=HT[:, kk, i * 128:(i + 1) * 128], rhs=W[:, kc * 512:(kc + 1) * 512],
                                start=(kg == 0 and kc == 0), stop=(kg == 3 and kc == 7)),
                                [wr, R("HT", kk)], [pr])
                        if kg == 3:
                            sc.op("dve", lambda e, i=i, hf=hf, pt=pt: e.tensor_tensor(
                                out=XA[:, i, hf * 512:(hf + 1) * 512], in0=pt[:, :], in1=XA[:, i, hf * 512:(hf + 1) * 512], op=ALU.add),
                                [pr, R("XA", i)], [R("XA", i)])
                    wprefetch()
            ln_group(NT, lambda i: store_rows(dst, p0 + i, i, R("xs", p0 + i) if dst is xs else R("y", p0 + i)))

        def a_head_load(l, t):
            p0 = t * NT
            e0 = 0 if t == 0 else 2
            head_load(xs, [min(max(p0 - 1 + e, 0), S - 1) for e in range(e0, 6)], "xs", k0=e0)

        def a_head_tr(l, t):
            if t > 0:
                sc.op("act", lambda e: e.activation(out=xT[:, :, 0:256], in_=xT[:, :, 512:768], func=AF.Copy), [R("xT")], [R("xT")])
                sc.op("act", lambda e: e.activation(out=KT[:, :, 0:256], in_=KT[:, :, 512:768], func=AF.Copy), [R("KT")], [R("KT")])
                sc.op("pool", lambda e: e.tensor_copy(out=VA[:, 0:2, :, 0:64], in_=VA[:, 4:6, :, 0:64]),
                      [R("VA", 4), R("VA", 5)], [R("VA", 0), R("VA", 1)])
                sc.op("pool", lambda e: e.tensor_copy(out=U[:, 0:2, :], in_=U[:, 4:6, :]),
                      [R("U", 4), R("U", 5)], [R("U", 0), R("U", 1)])
            head_tr(6, k0=0 if t == 0 else 2)

        def a_body1(l, t):
            p0 = t * NT
            if t == 0:
                load_vec(Gt, 5 * l + 0, "Gt")
                load_vec(Bt, 5 * l + 1, "Bt")
                sc.op("dve", lambda e: e.memset(QTz[:, :, :, :], 0.0), [], [R("QTz", jq) for jq in range(4)])
            e0 = 0 if t == 0 else 2
            W, wr = wnext(l, 0)
            for j in range(4):
                for (t0, n) in (((0, 512), (512, 256)) if t == 0 else ((256, 512),)):
                    bk = [0, 1, 6, 7, 2, 3, 4, 5][state["g"] % 8]
                    state["g"] += 1
                    pt, pr = psbank(bk)
                    for kc in range(8):
                        sc.op("pe", lambda e, kc=kc, j=j, t0=t0, n=n, W=W, pt=pt: e.matmul(
                            pt[:, 0:n], lhsT=W[:, kc * 512 + j * 128: kc * 512 + (j + 1) * 128],
                            rhs=xT[:, kc, t0:t0 + n], start=(kc == 0), stop=(kc == 7)),
                            [wr, R("xT")], [pr])
                    sc.op("act", lambda e, j=j, t0=t0, n=n, pt=pt: e.activation(out=KT[:, j, t0:t0 + n], in_=pt[:, 0:n], func=AF.Copy),
                          [pr], [R("KT")])
            wprefetch()
            W, wr = wnext(l, 1)
            for e_ in range(e0, 6):
                bk = [0, 1, 6, 7, 2, 3, 4, 5][state["g"] % 8]
                state["g"] += 1
                pt, pr = psbank(bk)
                for kc in range(8):
                    sc.op("pe", lambda e, kc=kc, e_=e_, W=W, pt=pt: e.matmul(
                        pt[:, :], lhsT=xT[:, kc, e_ * 128:(e_ + 1) * 128], rhs=W[:, kc * 512:(kc + 1) * 512],
                        start=(kc == 0), stop=(kc == 7)), [wr, R("xT")], [pr])
                sc.op("dve", lambda e, e_=e_, pt=pt: e.tensor_copy(out=U[:, e_, :], in_=pt[:, :]), [pr], [R("U", e_)])
            wprefetch()
            W, wr = wnext(l, 2)
            sc.op("act", lambda e, W=W: e.activation(out=WP[:], in_=W[:, 2048:2560], func=AF.Copy), [wr], [R("WP")])
            for e_ in range(e0, 6):
                bk = [0, 1, 6, 7, 2, 3, 4, 5][state["g"] % 8]
                state["g"] += 1
                pt, pr = psbank(bk)
                for kc in range(8):
                    sc.op("pe", lambda e, kc=kc, e_=e_, W=W, pt=pt: e.matmul(
                        pt[:, 0:256], lhsT=xT[:, kc, e_ * 128:(e_ + 1) * 128], rhs=W[:, kc * 256:(kc + 1) * 256],
                        start=(kc == 0), stop=(kc == 7)), [wr, R("xT")], [pr])
                sc.op("dve", lambda e, e_=e_, pt=pt: e.tensor_copy(
                    out=VA[:, e_, :, 0:64], in_=pt[:, 0:256].rearrange("p (j c) -> p j c", j=4)), [pr], [R("VA", e_)])
            if t == 0:
                bk = [0, 1, 6, 7, 2, 3, 4, 5][state["g"] % 8]
                state["g"] += 1
                pt, pr = psbank(bk)
                for kc in range(8):
                    sc.op("pe", lambda e, kc=kc, W=W, pt=pt: e.matmul(
                        pt[0:16, 0:256], lhsT=xT[:, kc, 128 + 112:256], rhs=W[:, kc * 256:(kc + 1) * 256],
                        start=(kc == 0), stop=(kc == 7)), [wr, R("xT")], [pr])
                sc.op("dve", lambda e, pt=pt: e.tensor_copy(
                    out=VAm[0:16, :, 0:64], in_=pt[0:16, 0:256].rearrange("p (j c) -> p j c", j=4)), [pr], [R("VAm")])
                sc.op("dve", lambda e: e.tensor_copy(out=KTm[:, :, :], in_=KT[:, :, 128 + 112:256]), [R("KT")], [R("KTm")])
            wprefetch()
            for qc in range(2):
                W, wr = wnext(l, 3 + qc)
                for mm in range(4):
                    m = qc * 4 + mm
                    bk = [0, 1, 6, 7, 2, 3, 4, 5][state["g"] % 8]
                    state["g"] += 1
                    pt, pr = psbank(bk)
                    for kc in range(8):
                        sc.op("pe", lambda e, kc=kc, mm=mm, W=W, pt=pt: e.matmul(
                            pt[:, :], lhsT=W[:, kc * 512 + mm * 128: kc * 512 + (mm + 1) * 128],
                            rhs=xT[:, kc, 128:640], start=(kc == 0), stop=(kc == 7)), [wr, R("xT")], [pr])
                    jq, hb = m // 2, 2 * (m % 2)
                    sc.op("act", lambda e, jq=jq, hb=hb, pt=pt: e.activation(
                        out=QTz[0:64, jq, :, hb * 128:(hb + 1) * 128], in_=pt[0:64, :].rearrange("p (i q) -> p i q", i=4),
                        func=AF.Copy), [pr], [R("QTz", jq)])
                    sc.op("act", lambda e, jq=jq, hb=hb, pt=pt: e.activation(
                        out=QTz[64:128, jq, :, (hb + 1) * 128:(hb + 2) * 128], in_=pt[64:128, :].rearrange("p (i q) -> p i q", i=4),
                        func=AF.Copy), [pr], [R("QTz", jq)])
                wprefetch()
            units = [(i, j) for i in range(NT) for j in range(4)]
            sctr = [0]

            def stage_a(u, i, j):
                pb = u % 3
                pos = p0 + i
                for c in range(4):
                    pa, pra = psbank(2 + sctr[0] % 4)
                    sctr[0] += 1
                    if c == 0:
                        nk = 16
                        lhs = KTm[:, j, :]
                        kres = R("KTm")
                    else:
                        nk = 128
                        e_ = i + c - 1
                        lhs = KT[:, j, e_ * 128:(e_ + 1) * 128]
                        kres = R("KT")
                    sc.op("pe", lambda e, pa=pa, lhs=lhs, nk=nk, c=c: e.matmul(
                        pa[0:nk, :], lhsT=lhs, rhs=QTz[:, j, i, :], start=True, stop=(c == 0)), [kres, R("QTz", j)], [pra])
                    if c == 0:
                        sc.op("act", lambda e, pa=pa: e.activation(
                            out=PT[pb][0:16, 0, :], in_=pa[0:16, :], func=AF.Exp, scale=0.125), [pra], [R("PT", pb, 0)])
                    else:
                        sc.op("pe", lambda e, pa=pa, c=c: e.matmul(
                            pa[:, :], lhsT=ident[:, :], rhs=alibi[:, c - 1, j, :], start=False, stop=True),
                            [R("ident"), R("alibi")], [pra])
                        sc.op("act", lambda e, pa=pa, c=c: e.activation(
                            out=PT[pb][:, c, :], in_=pa[:, :], func=AF.Exp, bias=pen[:, pos, c - 1:c], scale=0.125),
                            [pra, R("pen")], [R("PT", pb, c)])

            def stage_b(u, i, j):
                pb = u % 3
                rb = u % 2
                ib = i % 2
                po, pro = psbank(6 + u % 2)
                for blk in range(4):
                    for c in range(4):
                        if c == 0:
                            lhs = PT[pb][:, 0, blk * 128:(blk + 1) * 128]
                            rhs = VAm[:, j, 0:65]
                            rr = [R("VAm"), R("PT", pb), R("PT", pb, 0)]
                        else:
                            e_ = i + c - 1
                            lhs = PT[pb][:, c, blk * 128:(blk + 1) * 128]
                            rhs = VA[:, e_, j, 0:65]
                            rr = [R("VA", e_), R("VA"), R("PT", pb), R("PT", pb, c)]
                        sc.op("pe", lambda e, lhs=lhs, rhs=rhs, c=c, po=po, blk=blk: e.matmul(
                            po[:, blk * 65:(blk + 1) * 65], lhsT=lhs, rhs=rhs, start=(c == 0), stop=(c == 3)), rr, [pro])
                pov = po[:, 0:260].rearrange("p (b c) -> p b c", b=4)
                sc.op("dve", lambda e, pov=pov, rb=rb: e.tensor_tensor(
                    out=den[rb][:, :], in0=pov[:, :, 64], in1=esink[:, l * 16 + j * 4: l * 16 + j * 4 + 4], op=ALU.add),
                    [pro, R("esink")], [R("den", rb)])
                sc.op("dve", lambda e, rb=rb: e.reciprocal(out=rcp[rb][:, :], in_=den[rb][:, :]), [R("den", rb)], [R("rcp", rb)])
                for blk in range(4):
                    h = 4 * j + blk
                    sc.op("dve", lambda e, pov=pov, rb=rb, blk=blk, h=h, ib=ib: e.tensor_scalar(
                        out=ATtok[ib][:, h * 64:(h + 1) * 64], in0=pov[:, blk, 0:64], scalar1=rcp[rb][:, blk:blk + 1],
                        scalar2=None, op0=ALU.mult), [pro, R("rcp", rb)], [R("ATtok", ib)])
                if j == 3:
                    pending_tr.append([i, ib, 0])

            pending_tr = []

            def at_transposes(i, ib):
                if True:
                    pt, pr = psbank(state["g"] % 2)
                    state["g"] += 1
                    ptb = pt.bitcast(BF16)
                    for kc in range(8):
                        sc.op("pe", lambda e, kc=kc, ptb=ptb, ib=ib: e.transpose(
                            out=ptb[:, kc * 128:(kc + 1) * 128], in_=ATtok[ib][:, kc * 128:(kc + 1) * 128], identity=ident[:]),
                            [R("ATtok", ib), R("ident")], [pr])
                    sc.op("dve", lambda e, ptb=ptb: e.tensor_copy(
                        out=AT[:, :, i * 128:(i + 1) * 128], in_=ptb.rearrange("p (k t) -> p k t", k=8)),
                        [pr], [R("AT", k) for k in range(8)])

            def pool_blend(i):
                pos = p0 + i
                ab = i % 2
                if pos != 0:
                    sc.op("dve", lambda e, ab=ab, pos=pos: e.scalar_tensor_tensor(
                        out=Abl[ab][:, :, 120:128], in0=cp4(0, 32), scalar=nvp[:, pos:pos + 1], in1=cp4(32, 64),
                        op0=ALU.mult, op1=ALU.add), [R("cpool"), R("nvp")], [R("Abl", ab)])
                sc.op("dve", lambda e, ab=ab, pos=pos: e.tensor_scalar(
                    out=Anx[ab][:, :, :], in0=cp4(96, 128), scalar1=nvp[:, pos:pos + 1], scalar2=None, op0=ALU.mult),
                    [R("cpool"), R("nvp")], [R("Anx", ab)])

            def pool_slot(i):
                pos = p0 + i
                e_ = i + 1
                ab = i % 2
                if pos == 0:
                    aown = Ast
                    aown_r = R("Ast")
                else:
                    aown = Abl[ab]
                    aown_r = R("Abl", ab)
                pt, pr = psbank(state["g"] % 2)
                state["g"] += 1
                for g in range(4):
                    sc.op("pe", lambda e, g=g, aown=aown, pt=pt, e_=e_: e.matmul(
                        pt[:, g * 128:(g + 1) * 128], lhsT=U[:, e_, g * 128:(g + 1) * 128], rhs=aown[:, g, :],
                        start=True, stop=False), [R("U", e_), aown_r], [pr])
                    if pos != 0:
                        sc.op("pe", lambda e, g=g, pt=pt, e_=e_: e.matmul(
                            pt[:, g * 128:g * 128 + 8], lhsT=U[:, e_ - 1, g * 128:(g + 1) * 128], rhs=Apv[:, g, :],
                            start=False, stop=False), [R("U", e_ - 1), R("Apv")], [pr])
                    sc.op("pe", lambda e, g=g, pt=pt, e_=e_, ab=ab: e.matmul(
                        pt[:, g * 128 + 120:g * 128 + 128], lhsT=U[:, e_ + 1, g * 128:(g + 1) * 128], rhs=Anx[ab][:, g, :],
                        start=False, stop=True), [R("U", e_ + 1), R("Anx", ab)], [pr])
                sc.op("dve", lambda e, pt=pt, i=i: e.tensor_copy(
                    out=DT[:, :, i * 128:(i + 1) * 128], in_=pt[:, :].rearrange("p (g t) -> p g t", g=4)),
                    [pr], [R("DT")])
                if i + 1 < NT:
                    pool_blend(i + 1)

            wprefetch()
            if l == 0:
                convert(0, max(2, -(-NCH_M // ntiles)))
                convert(1, 1)
            else:
                convert(l + 1, 2)
            pool_blend(0)
            for u, (i, j) in enumerate(units):
                stage_a(u, i, j)
                for ent in pending_tr:
                    ent[2] += 1
                while pending_tr and pending_tr[0][2] >= 3:
                    ent = pending_tr.pop(0)
                    at_transposes(ent[0], ent[1])
                if u >= 2:
                    stage_b(u - 2, *units[u - 2])
                if u % 4 == 1:
                    pool_slot(u // 4)
            stage_b(len(units) - 2, *units[-2])
            stage_b(len(units) - 1, *units[-1])
            while pending_tr:
                ent = pending_tr.pop(0)
                at_transposes(ent[0], ent[1])
            for g in range(4):
                pt, pr = psbank(state["g"] % 2)
                state["g"] += 1
                sc.op("pe", lambda e, g=g, pt=pt: e.matmul(pt[:, :], lhsT=WP[:, g * 128:(g + 1) * 128], rhs=DT[:, g, :],
                                                           start=True, stop=True), [R("WP"), R("DT")], [pr])
                sc.op("act", lambda e, g=g, pt=pt: e.activation(out=YP[:, g, :], in_=pt[:, :], func=AF.Copy,
                                                                scale=pst[:, l * 4 + g: l * 4 + g + 1]),
                      [pr, R("pst")], [R("YP")])
            for m in range(8):
                W, wr = wnext(l, 5 + m)
                st = m % 2
                banks = [2, 3, 4, 5] if st == 0 else [6, 7, 0, 1]
                (pga, rga), (pya, rya), (pgb, rgb), (pyb, ryb) = [psbank(b) for b in banks]
                for kc in range(8):
                    sc.op("pe", lambda e, kc=kc, W=W, pga=pga: e.matmul(
                        pga[:, :], lhsT=W[:, kc * 128:(kc + 1) * 128], rhs=xT[:, kc, 128:640], start=(kc == 0), stop=(kc == 7)),
                        [wr, R("xT")], [rga])
                sc.op("act", lambda e, pga=pga, st=st: e.activation(out=TH[st][:, 0, :], in_=pga[:, :], func=AF.Tanh, scale=0.5),
                      [rga], [R("TH", st, 0)])
                for kc in range(8):
                    sc.op("pe", lambda e, kc=kc, W=W, pya=pya: e.matmul(
                        pya[:, :], lhsT=W[:, 2048 + kc * 128: 2048 + (kc + 1) * 128], rhs=AT[:, kc, :], start=(kc == 0), stop=(kc == 7)),
                        [wr, R("AT", kc)], [rya])
                sc.op("dve", lambda e, pya=pya, st=st: e.scalar_tensor_tensor(
                    out=TU[st][:], in0=TH[st][:, 0, :], scalar=1.0, in1=pya[:, :], op0=ALU.add, op1=ALU.mult),
                    [rya, R("TH", st, 0)], [R("TU", st)])
                for kc in range(8):
                    sc.op("pe", lambda e, kc=kc, W=W, pgb=pgb: e.matmul(
                        pgb[:, :], lhsT=W[:, 1024 + kc * 128: 1024 + (kc + 1) * 128], rhs=xT[:, kc, 128:640], start=(kc == 0), stop=(kc == 7)),
                        [wr, R("xT")], [rgb])
                sc.op("act", lambda e, pgb=pgb, st=st: e.activation(out=TH[st][:, 1, :], in_=pgb[:, :], func=AF.Tanh, scale=0.5),
                      [rgb], [R("TH", st, 1)])
                for kc in range(4):
                    sc.op("pe", lambda e, kc=kc, W=W, pyb=pyb: e.matmul(
                        pyb[:, :], lhsT=W[:, 3072 + kc * 128: 3072 + (kc + 1) * 128], rhs=YP[:, kc, :], start=(kc == 0), stop=(kc == 3)),
                        [wr, R("YP")], [ryb])
                sc.op("dve", lambda e, pyb=pyb, st=st: e.scalar_tensor_tensor(
                    out=TH[st][:, 1, :], in0=TH[st][:, 1, :], scalar=1.0, in1=pyb[:, :], op0=ALU.add, op1=ALU.mult),
                    [ryb, R("TH", st, 1)], [R("TH", st, 1)])
                sc.op("dve", lambda e, st=st, m=m: e.tensor_tensor(out=MT[:, m, :], in0=TU[st][:], in1=TH[st][:, 1, :], op=ALU.add),
                      [R("TU", st), R("TH", st, 1)], [R("MT", m)])
                wprefetch()

        def a_body2(l, t):
            p0 = t * NT
            for i in range(NT):
                pos = p0 + i
                sc.dma("pool", lambda e, pos=pos, i=i: e.dma_start(out=XA[:, i, :], in_=xs[pos * 128:(pos + 1) * 128, :]),
                       [R("xs", pos)], [R("XA", i)], f"xa{i}")
                sc.op("act", lambda e, i=i: e.activation(out=XA[:, i, :], in_=XA[:, i, :], func=AF.Copy, scale=ALPHA),
                      [R("XA", i)], [R("XA", i)])
            Ws = [wnext(l, 13), wnext(l, 14)]
            for i in range(NT):
                for hf in range(2):
                    W, wr = Ws[hf]
                    pt, pr = psbank([2, 3, 4, 5, 6, 7, 0, 1][state["g"] % 8])
                    state["g"] += 1
                    for kc in range(8):
                        sc.op("pe", lambda e, kc=kc, i=i, W=W, pt=pt: e.matmul(
                            pt[:, :], lhsT=MT[:, kc, i * 128:(i + 1) * 128], rhs=W[:, kc * 512:(kc + 1) * 512],
                            start=(kc == 0), stop=(kc == 7)), [wr, R("MT", kc)], [pr])
                    sc.op("dve", lambda e, i=i, hf=hf, pt=pt: e.scalar_tensor_tensor(
                        out=XA[:, i, hf * 512:(hf + 1) * 512], in0=pt[:, :], scalar=0.5,
                        in1=XA[:, i, hf * 512:(hf + 1) * 512], op0=ALU.mult, op1=ALU.add), [pr, R("XA", i)], [R("XA", i)])
            wprefetch()
            ln_group(NT, lambda i: store_rows(x1s, p0 + i, i, R("x1s", p0 + i)))

        load_consts()
        pass_emb()

        def head_ld(tk):
            (a_head_load if tk[0] == "A" else m_head_load)(tk[1], tk[2])

        def head_t(tk):
            (a_head_tr if tk[0] == "A" else m_head_tr)(tk[1], tk[2])

        def body1(tk):
            (a_body1 if tk[0] == "A" else m_body1)(tk[1], tk[2])

        def body2(tk):
            if tk[0] == "A":
                a_body2(tk[1], tk[2])
            else:
                m_body2(tk[1], tk[2], yout if tk[1] == L - 1 else xs)

        head_ld(tasks[0])
        head_t(tasks[0])
        for k, tk in enumerate(tasks):
            if k + 1 < len(tasks):
                head_ld(tasks[k + 1])
            if tk[0] == "M":
                done_in_a = min(NCH, (1 if tk[1] == 0 else 2) * ntiles)
                convert(tk[1] + 1, -(-(NCH - done_in_a) // ntiles))
            body1(tk)
            if k + 1 < len(tasks):
                head_t(tasks[k + 1])
            body2(tk)
        sc.final_wait("sp")

        with nc.Block() as block:
            sc.emit(nc, block, stack)
    return nc, sc


def _kcview(w):
    K, C = w.shape
    return w.reshape(K // 128, 128, C).transpose(1, 0, 2)


def build_wst(w_in, w_pool, w_bo_attn, w_bo_pool, w_out, w_mlp1, w_mlp2):
    L = w_in.shape[0]
    out = np.zeros((L, NCH, 128, CH), np.float32)
    for l in range(L):
        wi = _kcview(w_in[l])
        q, k, v, u = wi[:, :, 0:1024], wi[:, :, 1024:1280], wi[:, :, 1280:1536], wi[:, :, 1536:2048]
        ga, gb = wi[:, :, 2048:3072], wi[:, :, 3072:4096]
        kd = np.concatenate([k[:, :, j * 64:(j + 1) * 64] for j in range(4) for _ in range(2)], axis=2)
        out[l, 0] = kd.reshape(128, CH)
        out[l, 1] = u.reshape(128, CH)
        out[l, 2, :, 0:2048] = v.reshape(128, 2048)
        out[l, 2, :, 2048:2560] = w_pool[l].transpose(1, 0, 2).reshape(128, 512)
        out[l, 3] = q[:, :, 0:512].reshape(128, CH)
        out[l, 4] = q[:, :, 512:1024].reshape(128, CH)
        boa, bop = _kcview(w_bo_attn[l]), _kcview(w_bo_pool[l])
        for m in range(8):
            sl = slice(m * 128, (m + 1) * 128)
            out[l, 5 + m, :, 0:1024] = ga[:, :, sl].reshape(128, 1024)
            out[l, 5 + m, :, 1024:2048] = gb[:, :, sl].reshape(128, 1024)
            out[l, 5 + m, :, 2048:3072] = boa[:, :, sl].reshape(128, 1024)
            out[l, 5 + m, :, 3072:3584] = bop[:, :, sl].reshape(128, 512)
        wo = _kcview(w_out[l])
        out[l, 13] = wo[:, :, 0:512].reshape(128, CH)
        out[l, 14] = wo[:, :, 512:1024].reshape(128, CH)
        w1 = _kcview(w_mlp1[l])
        for c in range(8):
            out[l, NCH_A + c] = w1[:, :, c * 512:(c + 1) * 512].reshape(128, CH)
        w2 = _kcview(w_mlp2[l])
        for hf in range(2):
            for kg in range(4):
                out[l, NCH_A + 8 + hf * 4 + kg] = w2[:, kg * 8:(kg + 1) * 8, hf * 512:(hf + 1) * 512].reshape(128, CH)
    return out


def build_static_consts():
    slopes = (2.0 ** (-8.0 * np.arange(1, NH + 1) / NH)).astype(np.float32)
    jj = np.arange(128)[:, None].astype(np.float32)
    ii = np.arange(128)[None, :].astype(np.float32)
    alibi = np.zeros((128, 3, 4, 4, 128), np.float32)
    for j in range(4):
        for blk in range(4):
            h = 4 * j + blk
            s = slopes[h]
            alibi[:, 0, j, blk] = np.where(jj >= ii, -s * (128.0 + ii - jj), NEG)
            alibi[:, 1, j, blk] = -s * np.abs(jj - ii)
            alibi[:, 2, j, blk] = np.where(jj <= ii, -s * (128.0 + jj - ii), NEG)
    alibi = (8.0 * alibi).reshape(128, 3 * 4 * 512)

    def band(lo_clip=None, hi_clip=None):
        A = np.zeros((128, 4, 128), np.float64)
        for g, w in enumerate(POOL_W):
            for t in range(128):
                lo, hi = t - w // 2, t + w // 2
                lo_e = lo if lo_clip is None else max(lo, lo_clip)
                hi_e = hi if hi_clip is None else min(hi, hi_clip)
                if lo_clip is not None and t < lo_clip:
                    lo_e, hi_e = lo, hi
                cnt = hi_e - lo_e
                for s in range(max(lo_e, 0), min(hi_e, 128)):
                    A[s, g, t] += 1.0 / cnt
                A[t, g, t] -= 1.0
        return A

    A_norm = band()
    A_end = band(hi_clip=128)
    A_start = band(lo_clip=112)
    A_prev8 = np.zeros((128, 4, 8), np.float64)
    A_next8 = np.zeros((128, 4, 8), np.float64)
    for g, w in enumerate(POOL_W):
        for t in range(8):
            for sp_ in range(128):
                if sp_ - 128 >= t - w // 2:
                    A_prev8[sp_, g, t] = 1.0 / w
        for tt in range(8):
            t = 120 + tt
            for sp_ in range(128):
                if 128 + sp_ <= t + w // 2 - 1:
                    A_next8[sp_, g, tt] = 1.0 / w
    cpool = np.zeros((128, 1152), np.float32)
    cpool[:, 0:512] = A_norm.reshape(128, 512)
    cpool[:, 512:1024] = A_start.reshape(128, 512)
    cpool[:, 1024:1056] = (A_norm[:, :, 120:128] - A_end[:, :, 120:128]).reshape(128, 32)
    cpool[:, 1056:1088] = A_end[:, :, 120:128].reshape(128, 32)
    cpool[:, 1088:1120] = A_prev8.reshape(128, 32)
    cpool[:, 1120:1152] = A_next8.reshape(128, 32)
    return alibi, cpool


def chain_tables(blocks, ndummy, S):
    assert 1 + len(blocks) + ndummy == S
    pv = np.zeros(S, np.float32)
    ov = np.ones(S, np.float32)
    nv = np.zeros(S, np.float32)
    ov[0] = 0.0
    nv[0] = 1.0 if (len(blocks) and blocks[0] == 0) else 0.0
    for p in range(1, 1 + len(blocks)):
        b = blocks[p - 1]
        if p >= 2 and blocks[p - 2] == b - 1:
            pv[p] = 1.0
        if p < len(blocks) and blocks[p] == b + 1:
            nv[p] = 1.0
    pen = np.stack([pv, ov, nv], axis=1)
    pen = (pen - 1.0) * (-NEG)
    pen_t = np.broadcast_to(pen.reshape(1, S * 3), (128, S * 3)).astype(np.float32).copy()
    nvp_t = np.broadcast_to(nv.reshape(1, S), (128, S)).astype(np.float32).copy()
    return pen_t, nvp_t


def build_xin(x_seq, meta_tokens, blocks, ndummy, S):
    xin = np.zeros((S * 128, D), np.float32)
    xin[112:128] = meta_tokens
    for p, b in enumerate(blocks):
        xin[(p + 1) * 128:(p + 2) * 128] = x_seq[b * 128:(b + 1) * 128]
    return xin


def build_vec_consts(L, ln_emb_g, ln_emb_b, sink, pool_scale, ln1_g, ln1_b, b_mlp1, b_mlp2, ln2_g, ln2_b):
    c_vec = np.zeros((5 * L + 2, D), np.float32)
    for l in range(L):
        c_vec[5 * l + 0] = ln1_g[l]
        c_vec[5 * l + 1] = ln1_b[l]
        c_vec[5 * l + 2] = ln2_g[l]
        c_vec[5 * l + 3] = ln2_b[l]
        c_vec[5 * l + 4] = b_mlp2[l]
    c_vec[5 * L] = ln_emb_g
    c_vec[5 * L + 1] = ln_emb_b
    c_b1 = np.ascontiguousarray(b_mlp1[:L].reshape(L, 32, 128).transpose(2, 0, 1).reshape(128, L * 32))
    c_ps = np.ascontiguousarray(pool_scale[:L].reshape(L, 4, 128).transpose(2, 0, 1).reshape(128, L * 4))
    perm = np.arange(16)
    c_sink = np.ascontiguousarray(sink[:L][:, perm].reshape(1, L * 16))
    return c_vec, c_b1, c_ps, c_sink


S_FULL = 40
_PROG_CACHE = {}


def core_plan():
    plan = []
    for b in range(2):
        plan.append(("p", b, list(range(0, 36)), 3, list(range(0, 32))))
        plan.append(("p", b, [0, 1, 2] + list(range(28, 64)), 0, list(range(32, 64))))
    for b in range(4):
        plan.append(("s", b, list(range(32)), 7, list(range(32))))
    return plan


def kernel(x_prompt, x_sample, meta_tokens, ln_emb_g, ln_emb_b, w_in, sink, w_pool, pool_scale,
           w_bo_attn, w_bo_pool, w_out, ln1_g, ln1_b, w_mlp1, b_mlp1, w_mlp2, b_mlp2, ln2_g, ln2_b):
    f = lambda a: np.asarray(a, dtype=np.float32)
    x_prompt, x_sample, meta_tokens = f(x_prompt), f(x_sample), f(meta_tokens)
    L = DEPTH
    S = S_FULL
    if "nc" not in _PROG_CACHE:
        _PROG_CACHE["nc"] = build_program(S, L)[0]
    nc = _PROG_CACHE["nc"]
    wst = build_wst(f(w_in), f(w_pool), f(w_bo_attn), f(w_bo_pool), f(w_out), f(w_mlp1), f(w_mlp2))
    alibi, cpool = build_static_consts()
    c_vec, c_b1, c_ps, c_sink = build_vec_consts(L, f(ln_emb_g), f(ln_emb_b), f(sink), f(pool_scale), f(ln1_g), f(ln1_b),
                                                 f(b_mlp1), f(b_mlp2), f(ln2_g), f(ln2_b))
    plan = core_plan()
    in_maps = []
    for (src, b, blocks, ndummy, own) in plan:
        xseq = x_prompt[b] if src == "p" else x_sample[b]
        pen_t, nvp_t = chain_tables(blocks, ndummy, S)
        in_maps.append({
            "xin": build_xin(xseq, meta_tokens, blocks, ndummy, S), "wst": wst, "c_alibi": alibi, "c_pen": pen_t,
            "c_nvp": nvp_t, "c_pool": cpool, "c_vec": c_vec, "c_b1": c_b1, "c_ps": c_ps, "c_sink": c_sink,
        })
    res = run_bass_kernel_spmd(nc, in_maps, core_ids=list(range(8)))
    y_prompt = np.zeros(x_prompt.shape, np.float32)
    y_sample = np.zeros(x_sample.shape, np.float32)
    for ci, (src, b, blocks, ndummy, own) in enumerate(plan):
        y = res.results[ci]["y"]
        dst = y_prompt if src == "p" else y_sample
        for p, blk in enumerate(blocks):
            if blk in own:
                dst[b, blk * 128:(blk + 1) * 128] = y[(p + 1) * 128:(p + 2) * 128]
    return (y_prompt, y_sample)
```
